# Optimizing a Trainium2 kernel written in Bass

```python
import jax, jax.numpy as jnp
from jax import lax
import numpy as np

D_MODEL = 1024
BATCH = 16
SEQ = 2048
DEPTH = 1

CHUNK = 64
N_META = 16
Q_BLOCK = 128
FOX_HD = 64
FOX_HEADS = D_MODEL // 128
FOX_W = FOX_HEADS * FOX_HD
RET_DK = 128
RET_DV = 2 * RET_DK
RET_HEADS = D_MODEL // 256
RET_QK = RET_HEADS * RET_DK
RET_V = RET_HEADS * RET_DV
ROPE_BASE = 10000.0
D_FF = 2816
CONV_W = 3
EPS = 1e-6
NEG = -1e30

IN_SPLITS = [FOX_W, FOX_W, FOX_W, FOX_HEADS, RET_QK, RET_QK, RET_V, RET_V, D_MODEL, D_MODEL]
D_IN = int(sum(IN_SPLITS))
IN_OFFSETS = [int(o) for o in np.cumsum(IN_SPLITS)[:-1]]

kernel_name = "fox_retention_gated_hybrid_block"


def rms_norm(x, g):
    xf = x.astype(jnp.float32)
    y = xf * lax.rsqrt(jnp.mean(xf * xf, axis=-1, keepdims=True) + EPS)
    return (y * g.astype(jnp.float32)).astype(x.dtype)


def forgetting_attention(q, k, v, log_f):
    B, L, H, hd = q.shape
    Lp = -(-L // Q_BLOCK) * Q_BLOCK
    pad = Lp - L
    q, k, v = [jnp.pad(t, ((0, 0), (0, pad), (0, 0), (0, 0))) for t in (q, k, v)]
    F = jnp.cumsum(jnp.pad(log_f, ((0, 0), (0, pad), (0, 0))), axis=1)
    F = F.transpose(0, 2, 1)
    nb = Lp // Q_BLOCK
    qb = q.reshape(B, nb, Q_BLOCK, H, hd).transpose(1, 0, 3, 2, 4)
    Fq = F.reshape(B, H, nb, Q_BLOCK).transpose(2, 0, 1, 3)
    kpos = jnp.arange(Lp)
    scale = hd ** -0.5

    def block(args):
        qi, Fqi, i = args
        s = jnp.einsum('bhqd,bkhd->bhqk', qi, k).astype(jnp.float32) * scale
        s = s + Fqi[..., :, None] - F[:, :, None, :]
        qpos = i * Q_BLOCK + jnp.arange(Q_BLOCK)
        s = jnp.where(kpos[None, :] <= qpos[:, None], s, NEG)
        p = jax.nn.softmax(s, axis=-1).astype(v.dtype)
        return jnp.einsum('bhqk,bkhd->bqhd', p, v)

    out = lax.map(block, (qb, Fq, jnp.arange(nb)))
    out = out.transpose(1, 0, 2, 3, 4).reshape(B, Lp, H * hd)
    return out[:, :L]


def rotary(x, pos):
    d = x.shape[-1]
    inv = ROPE_BASE ** (-jnp.arange(0, d, 2, dtype=jnp.float32) / d)
    ang = pos.astype(jnp.float32)[:, None] * inv[None, :]
    cos = jnp.cos(ang)[None, :, None, :]
    sin = jnp.sin(ang)[None, :, None, :]
    x1 = x[..., : d // 2].astype(jnp.float32)
    x2 = x[..., d // 2:].astype(jnp.float32)
    return jnp.concatenate([x1 * cos - x2 * sin, x1 * sin + x2 * cos], axis=-1).astype(x.dtype)


def chunk_retention(q, k, v):
    B, L, H, dk = q.shape
    dv = v.shape[-1]
    front = (-N_META) % CHUNK
    Lr = L + front
    N = Lr // CHUNK

    def to_chunks(t):
        t = jnp.pad(t, ((0, 0), (front, 0), (0, 0), (0, 0)))
        return t.reshape(B, N, CHUNK, H, t.shape[-1]).transpose(1, 0, 3, 2, 4)

    qc, kc, vc = to_chunks(q), to_chunks(k), to_chunks(v)
    log_g = jnp.log(1.0 - 2.0 ** (-5.0 - jnp.arange(H, dtype=jnp.float32)))
    idx = jnp.arange(CHUNK, dtype=jnp.float32)
    intra_decay = jnp.exp(log_g[:, None, None] * jnp.abs(idx[:, None] - idx[None, :]))
    q_decay = jnp.exp(log_g[:, None] * (idx + 1.0))[..., None]
    k_decay = jnp.exp(log_g[:, None] * (CHUNK - 1.0 - idx))[..., None]
    chunk_decay = jnp.exp(log_g * CHUNK)[:, None, None]
    scale = dk ** -0.5

    def step(S, inp):
        qi, ki, vi = inp
        qf = qi.astype(jnp.float32) * scale
        kf = ki.astype(jnp.float32)
        vf = vi.astype(jnp.float32)
        a = jnp.einsum('bhid,bhjd->bhij', qf, kf) * intra_decay
        o = jnp.einsum('bhij,bhjv->bhiv', a, vf) + jnp.einsum('bhid,bhdv->bhiv', qf * q_decay, S)
        S = S * chunk_decay + jnp.einsum('bhjd,bhjv->bhdv', kf * k_decay, vf)
        return S, o

    S0 = jnp.zeros((B, H, dk, dv), jnp.float32)
    _, o = lax.scan(step, S0, (qc, kc, vc))
    o = o.transpose(1, 0, 3, 2, 4).reshape(B, Lr, H, dv)
    return o[:, front:]


def head_norm(o):
    mu = jnp.mean(o, axis=-1, keepdims=True)
    var = jnp.mean(jnp.square(o - mu), axis=-1, keepdims=True)
    return (o - mu) * lax.rsqrt(var + EPS)


def causal_depthwise_conv(u, w, b):
    C = u.shape[-1]
    y = lax.conv_general_dilated(
        u, w.reshape(CONV_W, 1, C).astype(u.dtype), window_strides=(1,),
        padding=[(CONV_W - 1, 0)], dimension_numbers=('NWC', 'WIO', 'NWC'),
        feature_group_count=C)
    return y + b


def hybrid_layer(x, norm_mix_g, w_in, b_forget, b_branch, w_fox_o, w_ret_o, w_out,
                 norm_ffn_g, w_up, conv_w, conv_b, w_down):
    B, L, _ = x.shape
    h = rms_norm(x, norm_mix_g)
    proj = h @ w_in
    fq, fk, fv, ff, rq, rk, rv, rg, ga, gr = jnp.split(proj, IN_OFFSETS, axis=-1)

    log_f = jax.nn.log_sigmoid((ff + b_forget).astype(jnp.float32))
    fox = forgetting_attention(fq.reshape(B, L, FOX_HEADS, FOX_HD),
                               fk.reshape(B, L, FOX_HEADS, FOX_HD),
                               fv.reshape(B, L, FOX_HEADS, FOX_HD), log_f)

    pos = jnp.arange(L)
    ret = chunk_retention(rotary(rq.reshape(B, L, RET_HEADS, RET_DK), pos),
                          rotary(rk.reshape(B, L, RET_HEADS, RET_DK), pos),
                          rv.reshape(B, L, RET_HEADS, RET_DV))
    ret = head_norm(ret).reshape(B, L, RET_V).astype(x.dtype) * jax.nn.silu(rg)

    gates = jax.nn.sigmoid(jnp.concatenate([ga, gr], axis=-1) + b_branch)
    g_a, g_r = jnp.split(gates, 2, axis=-1)
    merged = g_a * (fox @ w_fox_o) + g_r * (ret @ w_ret_o)
    x = x + merged @ w_out

    h2 = rms_norm(x, norm_ffn_g)
    u = causal_depthwise_conv(h2 @ w_up, conv_w, conv_b)
    u_gate, u_val = jnp.split(u, 2, axis=-1)
    x = x + (jax.nn.silu(u_gate) * u_val) @ w_down
    return x


def setup_inputs(seed: int = 0) -> dict:
    key = jax.random.key(seed)
    ks = jax.random.split(key, 16)
    f32 = jnp.float32
    nrm = lambda k, shape, fan_in: jax.random.normal(k, shape, f32) * (fan_in ** -0.5)
    return {
        "x": jax.random.normal(ks[0], (BATCH, SEQ, D_MODEL), f32),
        "meta": jax.random.normal(ks[1], (N_META, D_MODEL), f32),
        "norm_mix_g": 1.0 + 0.02 * jax.random.normal(ks[2], (DEPTH, D_MODEL), f32),
        "w_in": nrm(ks[3], (DEPTH, D_MODEL, D_IN), D_MODEL),
        "b_forget": jax.random.uniform(ks[4], (DEPTH, FOX_HEADS), f32, 1.0, 5.0),
        "b_branch": 0.02 * jax.random.normal(ks[5], (DEPTH, 2 * D_MODEL), f32),
        "w_fox_o": nrm(ks[6], (DEPTH, FOX_W, D_MODEL), FOX_W),
        "w_ret_o": nrm(ks[7], (DEPTH, RET_V, D_MODEL), RET_V),
        "w_out": nrm(ks[8], (DEPTH, D_MODEL, D_MODEL), D_MODEL),
        "norm_ffn_g": 1.0 + 0.02 * jax.random.normal(ks[9], (DEPTH, D_MODEL), f32),
        "w_up": nrm(ks[10], (DEPTH, D_MODEL, 2 * D_FF), D_MODEL),
        "conv_w": nrm(ks[11], (DEPTH, CONV_W, 2 * D_FF), CONV_W),
        "conv_b": 0.02 * jax.random.normal(ks[12], (DEPTH, 2 * D_FF), f32),
        "w_down": nrm(ks[13], (DEPTH, D_FF, D_MODEL), D_FF),
        "norm_f_g": 1.0 + 0.02 * jax.random.normal(ks[14], (D_MODEL,), f32),
    }


def reference(x, meta, norm_mix_g, w_in, b_forget, b_branch, w_fox_o, w_ret_o, w_out,
              norm_ffn_g, w_up, conv_w, conv_b, w_down, norm_f_g):
    B = x.shape[0]
    m = jnp.broadcast_to(meta[None].astype(x.dtype), (B, N_META, D_MODEL))
    h = jnp.concatenate([m, x], axis=1)
    for i in range(DEPTH):
        h = hybrid_layer(h, norm_mix_g[i], w_in[i], b_forget[i], b_branch[i], w_fox_o[i],
                         w_ret_o[i], w_out[i], norm_ffn_g[i], w_up[i], conv_w[i], conv_b[i],
                         w_down[i])
    h = rms_norm(h, norm_f_g)
    return h[:, N_META:]
```

```python
import contextlib
import numpy as np
import ml_dtypes
import concourse.bass as bass
import concourse.mybir as mybir
from concourse.bass_utils import run_bass_kernel_spmd

F32 = mybir.dt.float32
BF16 = mybir.dt.bfloat16
AF = mybir.ActivationFunctionType
ALU = mybir.AluOpType

D = 1024
BATCH = 16
SEQ = 2048
NMETA = 16
NCORES = 8
SPC = BATCH // NCORES
DIN = 6664
DFF = 2816
EPS = 1e-6
NT = 17
GROUPS = [[0, 1, 2, 3, 4], [5, 6, 7, 8], [9, 10, 11, 12], [13, 14, 15, 16]]
GMAX = 528
NEGM = -30000.0
WSLOT = 4096
NSLOT = 3


def tile_n(T):
    return NMETA if T == 0 else 128


def tile_pos(T):
    return 0 if T == 0 else NMETA + 128 * (T - 1)


class Res:
    __slots__ = ("name", "last_w", "readers", "excl")

    def __init__(self, name, excl=False):
        self.name = name
        self.last_w = None
        self.readers = []
        self.excl = excl


class Op:
    __slots__ = ("eng", "fn", "deps", "sig", "key", "val", "is_dma", "idx")


class Prog:
    ENG = ("pe", "act", "dve", "pool", "sp")

    def __init__(self, nc):
        self.nc = nc
        self.ops = []
        self.same_engine_sync = {"pe": False, "act": True, "dve": True, "pool": True, "sp": False}

    def res(self, name):
        return Res(name)

    def op(self, eng, meth, kw, reads=(), writes=(), dma_key=None):
        o = Op()
        o.eng = eng
        o.fn = (meth, kw)
        o.sig = False
        o.is_dma = dma_key is not None
        o.key = dma_key if o.is_dma else eng
        o.val = None
        o.idx = len(self.ops)
        deps = set()
        for r in reads:
            if r.last_w is not None:
                deps.add(r.last_w)
            if r.excl:
                for rd in r.readers:
                    if self.ops[rd].eng != eng:
                        deps.add(rd)
        for r in writes:
            if r.last_w is not None:
                deps.add(r.last_w)
            deps.update(r.readers)
        deps.discard(o.idx)
        o.deps = deps
        for r in reads:
            r.readers.append(o.idx)
        for r in writes:
            r.last_w = o.idx
            r.readers = []
        self.ops.append(o)
        return o

    def dma(self, eng, out_ap, in_ap, reads=(), writes=(), key=None):
        return self.op(eng, "dma_start", dict(out=out_ap, in_=in_ap), reads, writes, dma_key=key)

    def emit(self, final_wait_eng="sp"):
        nc = self.nc
        ops = self.ops
        for o in ops:
            need = set()
            for d in o.deps:
                p = ops[d]
                if (not p.is_dma) and p.eng == o.eng and not self.same_engine_sync[o.eng]:
                    continue
                need.add(d)
            o.deps = need
            for d in need:
                ops[d].sig = True
        for o in ops:
            if o.is_dma:
                o.sig = True
        last_of_eng = {}
        for o in ops:
            if not o.is_dma:
                last_of_eng[o.eng] = o
        for o in last_of_eng.values():
            o.sig = True
        counts = {}
        for o in ops:
            if o.sig:
                inc = 16 if o.is_dma else 1
                counts[o.key] = counts.get(o.key, 0) + inc
                o.val = counts[o.key]
        keys = sorted(counts.keys(), key=str)
        with contextlib.ExitStack() as st:
            sems = {k: st.enter_context(nc.semaphore("s_%s" % str(k).replace(" ", ""))) for k in keys}
            block = st.enter_context(nc.Block())
            per_eng = {e: [o for o in ops if o.eng == e] for e in self.ENG}

            def run_engine(ename, e):
                seen = {}
                for o in per_eng[ename]:
                    waits = {}
                    for d in o.deps:
                        p = ops[d]
                        if waits.get(p.key, 0) < p.val:
                            waits[p.key] = p.val
                    for k, v in waits.items():
                        if seen.get(k, 0) >= v:
                            continue
                        e.wait_ge(sems[k], v)
                        seen[k] = v
                    ins = getattr(e, o.fn[0])(**o.fn[1])
                    if o.sig:
                        ins.then_inc(sems[o.key], 16 if o.is_dma else 1)
                if ename == final_wait_eng:
                    for k in keys:
                        if k != ename:
                            e.wait_ge(sems[k], counts[k])

            @block.tensor
            def _(e):
                run_engine("pe", e)

            @block.scalar
            def _(e):
                run_engine("act", e)

            @block.vector
            def _(e):
                run_engine("dve", e)

            @block.gpsimd
            def _(e):
                run_engine("pool", e)

            @block.sync
            def _(e):
                run_engine("sp", e)


class Rot:
    def __init__(self, items):
        self.items = items
        self.i = 0

    def next(self):
        it = self.items[self.i % len(self.items)]
        self.i += 1
        return it


def make_consts():
    cf = {}
    tri = np.triu(np.ones((128, 128), np.float32))
    cf["tri"] = tri
    sel0 = np.zeros((128, 128), np.float32); sel0[0, :] = 1
    sel15 = np.zeros((128, 128), np.float32); sel15[15, :] = 1
    sel127 = np.zeros((128, 128), np.float32); sel127[127, :] = 1
    cf["sel0"], cf["sel15"], cf["sel127"] = sel0, sel15, sel127
    selq = np.zeros((128, 4, 4), np.float32)
    for qi in range(4):
        selq[64, qi, qi] = 1
    cf["selq"] = selq.reshape(128, 16)
    selm = np.zeros((128, 4), np.float32)
    selm[8, 0] = 1
    cf["selm"] = selm
    ones = np.zeros((128, 64), np.float32); ones[64, :] = 1
    cf["ones"] = ones
    hh = np.arange(4, dtype=np.float64)
    gam = 1.0 - 2.0 ** (-5.0 - hh)
    scale = 128.0 ** -0.5
    idx = np.arange(128, dtype=np.float64)
    qds = scale * gam[None, :] ** (idx[:, None] + 1.0)
    cf["qds"] = qds.astype(np.float32)
    cf["kd128"] = (gam[None, :] ** (127.0 - idx[:, None])).astype(np.float32)
    kd16 = np.ones((128, 4)); kd16[:16] = gam[None, :] ** (15.0 - idx[:16, None])
    cf["kd16"] = kd16.astype(np.float32)
    j = idx[:, None]; i = idx[None, :]
    M = np.zeros((128, 4, 128))
    allowed = (np.floor(j / 64) <= np.floor(i / 64))
    for h in range(4):
        M[:, h, :] = np.where(allowed, gam[h] ** (np.abs(i - j) - (i + 1.0)), 0.0)
    cf["M128"] = M.reshape(128, 512).astype(np.float32)
    M16 = np.zeros((128, 4, 16))
    for h in range(4):
        M16[:16, h, :] = gam[h] ** (np.abs(i[:, :16] - j[:16]) - (i[:, :16] + 1.0))
    cf["M16"] = M16.reshape(128, 64).astype(np.float32)
    cf["cd"] = [float(g ** 128.0) for g in gam]
    inv = 10000.0 ** (-np.arange(0, 128, 2, dtype=np.float64) / 128.0)
    cos = np.zeros((128, NT, 64)); sin = np.zeros((128, NT, 64))
    for T in range(NT):
        n = tile_n(T)
        pos = (tile_pos(T) + np.arange(n)).astype(np.float32).astype(np.float64)
        ang = (pos[:, None].astype(np.float32) * inv[None, :].astype(np.float32)).astype(np.float32)
        cos[:n, T] = np.cos(ang.astype(np.float64)); sin[:n, T] = np.sin(ang.astype(np.float64))
    cf["cos"] = cos.reshape(128, NT * 64).astype(np.float32)
    cf["sin"] = sin.reshape(128, NT * 64).astype(np.float32)
    cb = {}
    cb["ident"] = np.eye(128, dtype=np.float32).astype(ml_dtypes.bfloat16)
    maskT = np.where(np.arange(128)[:, None] <= np.arange(128)[None, :], 0.0, NEGM).astype(np.float32)
    cb["maskT"] = maskT.astype(ml_dtypes.bfloat16)
    ind = np.zeros((128, 512), np.float32)
    for r in range(4):
        for base in (0, 32, 64):
            ind[base + r, r * 128:(r + 1) * 128] = 1
    cb["ind"] = ind.astype(ml_dtypes.bfloat16)
    return cf, cb


CF_KEYS = ["tri", "sel0", "selq", "selm", "sel15", "sel127", "ones", "qds", "kd128", "kd16", "M128", "M16",
           "gmix", "gffn", "bfb", "bbr", "cw", "cbias"]
CF_W = {"tri": 128, "sel0": 128, "selq": 16, "selm": 4, "sel15": 128, "sel127": 128, "ones": 64, "qds": 4, "kd128": 4, "kd16": 4,
        "M128": 512, "M16": 64, "gmix": 8, "gffn": 8, "bfb": 8, "bbr": 16, "cw": 132, "cbias": 44}
CF_OFF = {}
_o = 0
for _k in CF_KEYS:
    CF_OFF[_k] = _o
    _o += CF_W[_k]
CF_TOT = _o


def build_nc(dbg=None):
    nc = bass.Bass("TRN2", target_bir_lowering=False)
    cfh, _ = make_consts()
    cd = cfh["cd"]

    def din(name, shape, dt=F32):
        return nc.dram_tensor(name, shape, dt, kind="ExternalInput").ap()

    x_d = din("x", [SPC, SEQ, D])
    meta_d = din("meta", [NMETA, D])
    w_in_d = din("w_in", [D, DIN])
    w_fo_d = din("w_fox_o", [512, D])
    w_ro_d = din("w_ret_o", [D, D])
    w_out_d = din("w_out", [D, D])
    w_up_d = din("w_up", [D, 2 * DFF])
    w_dn_d = din("w_down", [DFF, D])
    cf_d = din("cf", [128, CF_TOT])
    cb_d = din("cb", [128, 768], BF16)
    gf_d = din("gfb", [128, D])
    cos_d = din("cos", [128, NT * 64])
    sin_d = din("sin", [128, NT * 64])
    out_d = nc.dram_tensor("out", [SPC, SEQ, D], F32, kind="ExternalOutput").ap()

    P = Prog(nc)
    with contextlib.ExitStack() as st:
        def sb(name, shape, dt):
            return st.enter_context(nc.sbuf_tensor(name, shape, dt))

        def ps(name, shape, dt):
            return st.enter_context(nc.psum_tensor(name, shape, dt))

        cf = sb("cf_s", [128, CF_TOT], F32)
        cbt = sb("cbt", [128, 768], BF16)
        gfb = sb("gfb_s", [128, D], F32)
        cosg = sb("cosg", [128, 5 * 64], F32)
        sing = sb("sing", [128, 5 * 64], F32)
        xres = sb("xres", [128, 5, D], F32)
        actA = sb("actA", [128, 8, GMAX], BF16)
        fqT = sb("fqT", [128, 8, GMAX], BF16)
        kTc = sb("kTc", [128, 4, SEQ + NMETA], BF16)
        Vc = sb("Vc", [128, NT, 8, 65], BF16)
        Gc = sb("Gc", [128, NT, 8], F32)
        cpT = sb("cpT", [128, 8, 128], BF16)
        gpc = sb("gpc", [128, 64], F32)
        gpb = sb("gpb", [128, 32], BF16)
        big = sb("big", [128, 3 * 8 * GMAX], BF16)
        ar1 = sb("ar1", [128, 8960], F32)
        foxT = sb("foxT", [128, 4, GMAX], BF16)
        S32 = sb("S32", [128, 4, 256], F32)
        Sb = sb("Sb", [128, 4, 256], BF16)
        wsl = [sb("wsl%d" % i, [128, WSLOT], BF16) for i in range(NSLOT)]
        halo = sb("halo", [128, 44, 2], F32)
        stat = sb("stat", [128, 64], F32)
        xn2 = [sb("xn%d" % i, [128, D], BF16) for i in range(2)]
        r32 = [sb("r32_%d" % i, [128, 512], F32) for i in range(2)]
        tq = [sb("tq%d" % i, [128, 512], F32) for i in range(2)]
        ptb = [sb("ptb%d" % i, [128, 512], BF16) for i in range(4)]
        o32 = [sb("o32_%d" % i, [128, 512], F32) for i in range(2)]
        qkT = sb("qkT", [128, 8, 128], BF16)
        aMb = [sb("aM%d" % i, [128, 128], BF16) for i in range(4)]
        on16 = sb("on16", [128, D], BF16)
        junk = on16
        rtok = xn2
        lz = sb("lz", [128, 16], F32)
        lzt = sb("lzt", [128, 5, 16], F32)
        gref = sb("gref", [128, 8], F32)
        bnst = sb("bnst", [128, 4, 6], F32)
        bnmv = sb("bnmv", [128, 4, 2], F32)
        hrs = sb("hrs", [128, 16], F32)

        gaT = big[:, 0:8 * GMAX].rearrange("p (c n) -> p c n", c=8)
        grT = big[:, 8 * GMAX:16 * GMAX].rearrange("p (c n) -> p c n", c=8)
        retT = big[:, 16 * GMAX:24 * GMAX].rearrange("p (c n) -> p c n", c=8)
        actT = big[:, 0:22 * GMAX].rearrange("p (c n) -> p c n", c=22)
        ar1b = ar1.bitcast(BF16)
        rq = ar1b[:, 0:2560].rearrange("p (t n) -> p t n", t=5)
        rk = ar1b[:, 2560:5120].rearrange("p (t n) -> p t n", t=5)
        kdk = ar1b[:, 5120:7680].rearrange("p (t n) -> p t n", t=5)
        rv = ar1b[:, 7680:12800].rearrange("p (t n) -> p t n", t=5)
        rgs = ar1b[:, 12800:17920].rearrange("p (t n) -> p t n", t=5)
        m12 = [ar1[:, i * 512:(i + 1) * 512] for i in range(4)]
        upe = [ar1[:, 2048 + i * 544: 2048 + (i + 1) * 544] for i in range(3)]
        y0b = [ar1[:, 3680 + i * 544: 3680 + (i + 1) * 544] for i in range(4)]
        outb = [ar1[:, i * 1024:(i + 1) * 1024] for i in range(5)]
        xnD = [ar1b[:, 15808 + i * 1024: 15808 + (i + 1) * 1024] for i in range(2)]
        y2b = [ar1b[:, 11712 + i * 1088: 11712 + (i + 1) * 1088] for i in range(3)]

        pmm = [ps("pmm%d" % i, [128, 512], F32) for i in range(3)]
        pT = ps("pT", [128, 8, 128], BF16)
        pSk = [ps("pS%d" % i, [128, 512], F32) for i in range(2)]
        pOk = [ps("pO%d" % i, [128, 512], F32) for i in range(2)]

        R = P.res
        r_const = R("const")
        r_cs = R("cossin")
        r_x = [R("x%d" % i) for i in range(5)]
        r_hT = [R("hT%d" % i) for i in range(5)]
        r_fq = [R("fq%d" % p) for p in range(4)]
        r_kT = [[R("kT%d_%d" % (p, g)) for g in range(4)] for p in range(4)]
        r_V = [R("V%d" % g) for g in range(4)]
        r_G = [R("G%d" % T) for T in range(NT)]
        r_bias = R("biasJ")
        r_ga = [R("ga%d" % c) for c in range(8)]
        r_gr = [R("gr%d" % c) for c in range(8)]
        r_ret = [R("retT%d" % i) for i in range(5)]
        r_act = [R("actT%d" % f) for f in range(22)]
        r_rq = [R("rq%d" % i) for i in range(5)]
        r_rk = [R("rk%d" % i) for i in range(5)]
        r_kdk = [R("kdk%d" % i) for i in range(5)]
        r_rv = [R("rv%d" % i) for i in range(5)]
        r_rgs = [R("rgs%d" % i) for i in range(5)]
        r_fox = [R("fox%d" % i) for i in range(5)]
        r_S32 = [R("S32_%d" % h) for h in range(4)]
        r_Sb = [R("Sb%d" % h) for h in range(4)]
        r_w = [R("w%d" % i) for i in range(NSLOT)]
        r_halo = [R("halo%d" % b) for b in range(44)]
        r_stat = [R("stat%d" % i) for i in range(8)]
        r_m12 = [R("m12_%d" % i) for i in range(4)]
        r_upe = [R("upe%d" % i) for i in range(3)]
        r_uph = [R("uph%d" % i) for i in range(3)]
        r_y0 = [R("y0_%d" % i) for i in range(4)]
        r_y2b = [R("y2b%d" % i) for i in range(3)]
        r_outb = [R("outb%d" % i) for i in range(5)]
        r_out = R("outdram")
        r_xnD = [R("xnD%d" % i) for i in range(2)]

        xnR = Rot([(xn2[i], R("xn%d" % i)) for i in range(2)])
        xnDR = Rot([(xnD[i], r_xnD[i]) for i in range(2)])
        r32R = Rot([(r32[i], R("r32_%d" % i)) for i in range(2)])
        tqR = Rot([(tq[i], R("tq%d" % i)) for i in range(2)])
        ptR = Rot([(ptb[i], R("pt%d" % i)) for i in range(4)])
        o32R = Rot([(o32[i], R("o32_%d" % i)) for i in range(2)])
        aMR = Rot([(aMb[i], R("aM%d" % i)) for i in range(4)])
        r_qkT, r_on32, r_lz, r_gref, r_bn = R("qkT"), R("on32"), R("lz"), R("gref"), R("bn")
        rtokR = Rot([(rtok[i], xnR.items[i][1]) for i in range(2)])
        XR = lambda nm: Res(nm, excl=True)
        pmmR = Rot([(pmm[i], XR("pmm%d" % i)) for i in range(3)])
        r_pT = XR("pT")
        r_pS = [XR("pS%d" % i) for i in range(2)]
        r_pO = [XR("pO%d" % i) for i in range(2)]
        pcum = pSk[0][:, 0:8]
        pgref = pSk[0][:, 8:16]
        r_pcum = r_pS[0]
        r_pgref = r_pS[0]
        pRA = pSk[0][:, 128:256]
        r_pRA = r_pS[0]
        pSt = pSk[1][:, 0:256]
        r_pSt = r_pS[1]
        statR = Rot([(stat[:, 8 * i:8 * i + 8], r_stat[i]) for i in range(8)])

        def C(key, lo=0, hi=None):
            w = CF_W[key]
            hi = w if hi is None else hi
            return cf[:, CF_OFF[key] + lo:CF_OFF[key] + hi]

        identb = cbt[:, 0:128]
        maskT = cbt[:, 128:256]
        indq = cbt[:, 256:768]

        P.dma("sp", cf[:], cf_d, writes=[r_const], key="c0")
        P.dma("sp", cbt[:], cb_d, writes=[r_const], key="c1")
        P.dma("sp", gfb[:], gf_d, writes=[r_const], key="c2")
        P.op("dve", "memset", dict(ap=Vc[:], constant=1.0), writes=r_V)
        P.op("dve", "memset", dict(ap=Vc[:, 0, :, :], constant=0.0), writes=r_V)
        P.op("dve", "memset", dict(ap=Vc[0:16, 0, :, 64:65], constant=1.0), writes=r_V)
        for pi_ in range(4):
            P.op("dve", "memset", dict(ap=ptb[pi_][:], constant=0.0), writes=[ptR.items[pi_][1]])
        P.op("dve", "memset", dict(ap=cpT[:], constant=0.0), writes=[r_bias])
        P.op("dve", "memset", dict(ap=fqT[:], constant=0.0), writes=r_fq)
        for o_i in range(2):
            P.op("dve", "memset", dict(ap=o32[o_i][:], constant=0.0), writes=[o32R.items[o_i][1]])

        def wdesc_list():
            L_ = []
            secs = [0, 512, 1024, 1536, 1544, 2056, 2568, 3080, 3592, 4104, 4616, 5128, 5640, 6152, 6664]
            for i in [2, 0, 1, 3, 4, 5, 6, 7, 8, 9, 10, 11, 12, 13]:
                c0, c1 = secs[i], secs[i + 1]
                L_.append((w_in_d[:, c0:c1].rearrange("(c p) n -> p c n", p=128), 8, c1 - c0))
            L_.append((w_fo_d.rearrange("(c p) n -> p c n", p=128), 4, 1024))
            for hf in range(2):
                L_.append((w_ro_d[:, hf * 512:(hf + 1) * 512].rearrange("(c p) n -> p c n", p=128), 8, 512))
            for hf in range(2):
                L_.append((w_out_d[:, hf * 512:(hf + 1) * 512].rearrange("(c p) n -> p c n", p=128), 8, 512))
            for i in range(11):
                L_.append((w_up_d[:, i * 512:(i + 1) * 512].rearrange("(c p) n -> p c n", p=128), 8, 512))
            for i in range(8):
                L_.append((w_dn_d[:, i * 128:(i + 1) * 128].rearrange("(c p) n -> p c n", p=128), 22, 128))
            return L_

        wdesc = wdesc_list()
        NCH = len(wdesc)
        NGRP = SPC * 4
        total_chunks = NCH * NGRP
        wstate = {"issued": 0, "cur": 0}

        wscr = nc.dram_tensor("wscr", [NCH, 128, WSLOT], BF16, kind="Internal").ap()
        r_scr = [R("scr%d" % i) for i in range(NCH)]

        def w_issue_upto(k):
            while wstate["issued"] <= min(k, total_chunks - 1):
                i = wstate["issued"]
                ci = i % NCH
                src, kc, ncol = wdesc[ci]
                s = i % NSLOT
                if i < NCH:
                    dst = wsl[s][:, 0:kc * ncol].rearrange("p (c n) -> p c n", c=kc)
                    P.dma("pool", dst, src, writes=[r_w[s]], key=("w", s))
                    if NGRP > 1:
                        P.dma("sp", wscr[ci, :, 0:kc * ncol], wsl[s][:, 0:kc * ncol], reads=[r_w[s]], writes=[r_scr[ci]], key=("ws", s))
                else:
                    P.dma("pool", wsl[s][:, 0:kc * ncol], wscr[ci, :, 0:kc * ncol], reads=[r_scr[ci]], writes=[r_w[s]], key=("w", s))
                wstate["issued"] += 1

        def w_next():
            i = wstate["cur"]
            w_issue_upto(i + NSLOT - 1)
            src, kc, ncol = wdesc[i % NCH]
            s = i % NSLOT
            wstate["cur"] += 1
            return wsl[s][:, 0:kc * ncol].rearrange("p (c n) -> p c n", c=kc), r_w[s]

        def mm(out, lhsT, rhs, start, stop, reads, writes):
            P.op("pe", "matmul", dict(out=out, lhsT=lhsT, rhs=rhs, start=start, stop=stop), reads=reads, writes=writes)

        def transp(out, in_, n, reads, writes):
            P.op("pe", "transpose", dict(out=out, in_=in_, identity=identb[0:n, 0:n]),
                 reads=list(reads) + [r_const], writes=writes)

        def rmsnorm_T(ti, n, c0, gkey, xrot):
            st_, rst = statR.next()
            P.op("act", "activation", dict(out=junk[0:n, :], in_=xres[0:n, ti, :], func=AF.Square,
                                               accum_out=st_[0:n, 0:1]), reads=[r_x[ti]], writes=[rst, r_on32])
            P.op("act", "activation", dict(out=st_[0:n, 1:2], in_=st_[0:n, 0:1], func=AF.Sqrt, scale=1.0 / D, bias=EPS),
                 reads=[rst], writes=[rst])
            P.op("dve", "reciprocal", dict(out=st_[0:n, 2:3], in_=st_[0:n, 1:2]), reads=[rst], writes=[rst])
            xn, rxn = xrot.next()
            P.op("dve", "tensor_scalar", dict(out=xn[0:n, :], in0=xres[0:n, ti, :], scalar1=st_[0:n, 2:3], scalar2=None,
                                                  op0=ALU.mult), reads=[r_x[ti], rst], writes=[rxn])
            for c in range(8):
                transp(pT[:, c, 0:n], xn[0:n, c * 128:(c + 1) * 128], n, [rxn], [r_pT])
            g = C(gkey)
            P.op("dve", "tensor_tensor", dict(out=actA[:, :, c0:c0 + n], in0=pT[:, :, 0:n],
                                                  in1=g.unsqueeze(2).to_broadcast([128, 8, n]), op=ALU.mult),
                 reads=[r_pT, r_const], writes=[r_hT[ti]])

        dbg_outs = []

        def dump(name, ap, shape, dt, reads):
            if dbg is None or name not in dbg:
                return
            d = nc.dram_tensor("dbg_" + name, shape, dt, kind="ExternalOutput").ap()
            P.dma("sp", d, ap, reads=reads, writes=[R("dbg" + name)], key="dbg_" + name)
            dbg_outs.append(name)

        gcount = 0
        all_groups = [(sq_, g_) for sq_ in range(SPC) for g_ in range(len(GROUPS))]
        prefetched = set()

        def xload(seq_, T_, slot):
            n_ = tile_n(T_)
            src = meta_d if T_ == 0 else x_d[seq_, 128 * (T_ - 1):128 * T_, :]
            P.dma("sp", xres[0:n_, slot, :], src, writes=[r_x[slot]], key=("x", slot))

        for seq in range(SPC):
            P.op("dve", "memset", dict(ap=halo[:], constant=0.0), writes=r_halo)
            for gi, grp in enumerate(GROUPS):
                if dbg is not None and gcount >= dbg.get("ngroups", 99):
                    break
                gcount += 1
                nt = len(grp)
                ns = [tile_n(T) for T in grp]
                col0 = [sum(ns[:i]) for i in range(nt)]
                G = sum(ns)
                if gi == 0:
                    segs = [(0, NMETA, [0]), (NMETA, G - NMETA, [1, 2, 3, 4])]
                else:
                    segs = [(0, G, [0, 1, 2, 3])]
                kv0 = tile_pos(grp[0])

                P.op("pool", "memset", dict(ap=stat[:, 60:61], constant=0.0),
                     writes=r_m12 + r_upe + r_uph + r_y0 + r_y2b + r_outb + r_xnD + r_rq + r_rk + r_kdk + r_rv + r_rgs + r_act + r_ga + r_gr + r_ret)

                T0 = grp[0]
                P.dma("sp", cosg[:, 0:nt * 64], cos_d[:, T0 * 64:(T0 + nt) * 64], writes=[r_cs], key="cs0")
                P.dma("sp", sing[:, 0:nt * 64], sin_d[:, T0 * 64:(T0 + nt) * 64], writes=[r_cs], key="cs1")
                for ti, T in enumerate(grp):
                    if ti not in prefetched:
                        xload(seq, T, ti)
                prefetched.clear()
                for ti, T in enumerate(grp):
                    rmsnorm_T(ti, ns[ti], col0[ti], "gmix", xnR)

                def fm_proj(W, rW, kc, wc0, src, src_res_of_seg, evac, rot=None):
                    for (s0, sl, tis) in segs:
                        pb, rpb = (rot or pmmR).next()
                        rr = [rW] + src_res_of_seg(tis)
                        for c in range(kc):
                            mm(pb[:, 0:sl], W[:, c, wc0:wc0 + 128], src[:, c, s0:s0 + sl], c == 0, c == kc - 1, rr, [rpb])
                        evac(pb, rpb, s0, sl, tis)

                def tm_proj(W, rW, kc, ncol, src, src_res, ti, evac, wc0=0):
                    n = ns[ti]
                    pb, rpb = pmmR.next()
                    for c in range(kc):
                        mm(pb[0:n, 0:ncol], src[:, c, col0[ti]:col0[ti] + n], W[:, c, wc0:wc0 + ncol], c == 0, c == kc - 1,
                           [rW, src_res], [rpb])
                    evac(pb, rpb, n)

                hres = lambda tis: [r_hT[t] for t in tis]

                W, rW = w_next()
                for ti, T in enumerate(grp):
                    def ev(pb, rpb, n, T=T):
                        P.op("act", "activation", dict(out=Vc[0:n, T, :, 0:64],
                                                           in_=pb[0:n, 0:512].rearrange("p (h d) -> p h d", h=8), func=AF.Copy),
                             reads=[rpb], writes=[r_V[gi]])
                    tm_proj(W, rW, 8, 512, actA, r_hT[ti], ti, ev)
                W, rW = w_next()
                for p in range(4):
                    def ev(pb, rpb, s0, sl, tis, p=p):
                        P.op("act", "activation", dict(out=fqT[0:64, 2 * p, s0:s0 + sl], in_=pb[0:64, 0:sl], func=AF.Copy),
                             reads=[rpb], writes=[r_fq[p]])
                        P.op("act", "activation", dict(out=fqT[64:128, 2 * p + 1, s0:s0 + sl], in_=pb[64:128, 0:sl], func=AF.Copy),
                             reads=[rpb], writes=[r_fq[p]])
                    fm_proj(W, rW, 8, p * 128, actA, hres, ev)
                W, rW = w_next()
                for p in range(4):
                    def ev(pb, rpb, s0, sl, tis, p=p):
                        P.op("act", "activation", dict(out=kTc[:, p, kv0 + s0:kv0 + s0 + sl], in_=pb[:, 0:sl], func=AF.Copy),
                             reads=[rpb], writes=[r_kT[p][gi]])
                    fm_proj(W, rW, 8, p * 128, actA, hres, ev)
                W, rW = w_next()
                r_lzt = [R("lzt%d" % i) for i in range(5)]
                for ti, T in enumerate(grp):
                    def ev(pb, rpb, n, T=T, ti=ti):
                        P.op("dve", "tensor_tensor", dict(out=lzt[0:n, ti, 0:8], in0=pb[0:n, 0:8], in1=C("bfb")[0:n, :], op=ALU.add),
                             reads=[rpb, r_const], writes=[r_lzt[ti]])
                        P.op("act", "activation", dict(out=lzt[0:n, ti, 0:8], in_=lzt[0:n, ti, 0:8], func=AF.Exp, scale=-1.0),
                             reads=[r_lzt[ti]], writes=[r_lzt[ti]])
                        P.op("act", "activation", dict(out=lzt[0:n, ti, 8:16], in_=lzt[0:n, ti, 0:8], func=AF.Ln, bias=1.0),
                             reads=[r_lzt[ti]], writes=[r_lzt[ti]])
                    tm_proj(W, rW, 8, 8, actA, r_hT[ti], ti, ev)

                def cumsum_tile(ti, T):
                    n = ns[ti]
                    first = (T == 0)
                    mm(pcum[0:n, :], C("tri")[0:n, 0:n], lzt[0:n, ti, 8:16], True, first, [r_lzt[ti], r_const], [r_pcum])
                    if not first:
                        npv = tile_n(T - 1)
                        sel = C("sel15") if npv == 16 else C("sel127")
                        mm(pcum[0:n, :], sel[0:npv, 0:n], Gc[0:npv, T - 1, :], False, True, [r_G[T - 1], r_const], [r_pcum])
                    P.op("dve", "tensor_copy", dict(out=Gc[0:n, T, :], in_=pcum[0:n, :]), reads=[r_pcum], writes=[r_G[T]])

                def rotary(pb, rpb, n, ti):
                    ps8 = pb[0:n, 0:512].rearrange("p (g d) -> p g d", g=8)
                    cb_ = cosg[0:n, ti * 64:(ti + 1) * 64].unsqueeze(1).to_broadcast([n, 8, 64])
                    sb_ = sing[0:n, ti * 64:(ti + 1) * 64].unsqueeze(1).to_broadcast([n, 8, 64])
                    r_, rr_ = r32R.next()
                    (ta, rta), (tb, rtb) = tqR.next(), tqR.next()
                    P.op("dve", "tensor_tensor", dict(out=ta[0:n, :].rearrange("p (g d) -> p g d", g=8), in0=ps8, in1=cb_, op=ALU.mult),
                         reads=[rpb, r_cs], writes=[rta])
                    P.op("dve", "tensor_tensor", dict(out=tb[0:n, :].rearrange("p (g d) -> p g d", g=8), in0=ps8, in1=sb_, op=ALU.mult),
                         reads=[rpb, r_cs], writes=[rtb])
                    A4 = ta[0:n, :].rearrange("p (h t d) -> p h t d", h=4, t=2)
                    B4 = tb[0:n, :].rearrange("p (h t d) -> p h t d", h=4, t=2)
                    r4 = r_[0:n, :].rearrange("p (h t d) -> p h t d", h=4, t=2)
                    P.op("dve", "tensor_tensor", dict(out=r4[:, :, 0, :], in0=A4[:, :, 0, :], in1=B4[:, :, 1, :], op=ALU.subtract),
                         reads=[rta, rtb], writes=[rr_])
                    P.op("dve", "tensor_tensor", dict(out=r4[:, :, 1, :], in0=B4[:, :, 0, :], in1=A4[:, :, 1, :], op=ALU.add),
                         reads=[rta, rtb], writes=[rr_])
                    return r_, rr_

                W, rW = w_next()
                for ti, T in enumerate(grp):
                    def ev(pb, rpb, n, ti=ti):
                        r_, rr_ = rotary(pb, rpb, n, ti)
                        for h in range(4):
                            P.op("act", "activation", dict(out=rq[0:n, ti, h * 128:(h + 1) * 128], in_=r_[0:n, h * 128:(h + 1) * 128],
                                                           func=AF.Copy, scale=C("qds")[0:n, h:h + 1]),
                                 reads=[rr_, r_const], writes=[r_rq[ti]])
                    tm_proj(W, rW, 8, 512, actA, r_hT[ti], ti, ev)
                W, rW = w_next()
                for ti, T in enumerate(grp):
                    def ev(pb, rpb, n, ti=ti, T=T):
                        r_, rr_ = rotary(pb, rpb, n, ti)
                        P.op("act", "activation", dict(out=rk[0:n, ti, :], in_=r_[0:n, :], func=AF.Copy),
                             reads=[rr_], writes=[r_rk[ti]])
                        kd = C("kd16") if T == 0 else C("kd128")
                        for h in range(4):
                            P.op("act", "activation", dict(out=kdk[0:n, ti, h * 128:(h + 1) * 128], in_=r_[0:n, h * 128:(h + 1) * 128],
                                                           func=AF.Copy, scale=kd[0:n, h:h + 1]),
                                 reads=[rr_, r_const], writes=[r_kdk[ti]])
                    tm_proj(W, rW, 8, 512, actA, r_hT[ti], ti, ev)
                    cumsum_tile(ti, T)
                for hf in range(2):
                    W, rW = w_next()
                    for ti, T in enumerate(grp):
                        def ev(pb, rpb, n, ti=ti, hf=hf):
                            P.op("act", "activation", dict(out=rv[0:n, ti, hf * 512:(hf + 1) * 512], in_=pb[0:n, 0:512], func=AF.Copy),
                                 reads=[rpb], writes=[r_rv[ti]])
                        tm_proj(W, rW, 8, 512, actA, r_hT[ti], ti, ev)
                for hf in range(2):
                    W, rW = w_next()
                    for ti, T in enumerate(grp):
                        def ev(pb, rpb, n, ti=ti, hf=hf):
                            P.op("act", "activation", dict(out=rgs[0:n, ti, hf * 512:(hf + 1) * 512], in_=pb[0:n, 0:512], func=AF.Silu),
                                 reads=[rpb], writes=[r_rgs[ti]])
                        tm_proj(W, rW, 8, 512, actA, r_hT[ti], ti, ev)
                def ret_stage1(ti_):
                    n_ = ns[ti_]
                    for h in range(4):
                        transp(pT[:, h, 0:n_], rq[0:n_, ti_, h * 128:(h + 1) * 128], n_, [r_rq[ti_]], [r_pT])
                    for h in range(4):
                        transp(pT[:, 4 + h, 0:n_], rk[0:n_, ti_, h * 128:(h + 1) * 128], n_, [r_rk[ti_]], [r_pT])
                    P.op("act", "activation", dict(out=qkT[:, :, 0:n_], in_=pT[:, :, 0:n_], func=AF.Copy), reads=[r_pT], writes=[r_qkT])

                ret_state = {"pend": None}

                def make_ret(ti, T):
                    n = ns[ti]
                    c0 = col0[ti]
                    Mt = C("M16") if T == 0 else C("M128")
                    Mt3 = Mt.rearrange("p (h i) -> p h i", h=4)
                    bankC, rC = pmmR.items[2]
                    por = [(pOk[0], r_pO[0]), (pOk[1], r_pO[1])]
                    Xb = [(pSk[0], r_pS[0]), (pSk[1], r_pS[1])]
                    vvs = [rv[0:n, ti, h * 256:(h + 1) * 256] for h in range(4)]
                    aMs = {}

                    def St(h):
                        mm(bankC[:, (h % 2) * 256:(h % 2) * 256 + 256], kdk[0:n, ti, h * 128:(h + 1) * 128], vvs[h], True, True,
                           [r_kdk[ti], r_rv[ti]], [rC])

                    def Supd(h):
                        src = bankC[:, (h % 2) * 256:(h % 2) * 256 + 256]
                        if T == 0:
                            P.op("dve", "tensor_copy", dict(out=S32[:, h, :], in_=src), reads=[rC], writes=[r_S32[h]])
                        else:
                            P.op("dve", "scalar_tensor_tensor", dict(out=S32[:, h, :], in0=S32[:, h, :], scalar=cd[h], in1=src,
                                                                     op0=ALU.mult, op1=ALU.add), reads=[rC, r_S32[h]], writes=[r_S32[h]])
                        P.op("act", "activation", dict(out=Sb[:, h, :], in_=S32[:, h, :], func=AF.Copy), reads=[r_S32[h]], writes=[r_Sb[h]])

                    def aT(h):
                        pa, rpa = Xb[h % 2]
                        mm(pa[0:n, 0:n], qkT[:, 4 + h, 0:n], qkT[:, h, 0:n], True, True, [r_qkT], [rpa])

                    def aMk(h):
                        pa, rpa = Xb[h % 2]
                        aM, raM = aMR.next()
                        aMs[h] = (aM, raM)
                        P.op("dve", "tensor_tensor", dict(out=aM[0:n, 0:n], in0=pa[0:n, 0:n], in1=Mt3[0:n, h, 0:n], op=ALU.mult),
                             reads=[rpa, r_const], writes=[raM])

                    def omm(h):
                        aM, raM = aMs[h]
                        pb, rpb = por[h // 2]
                        oc = (h % 2) * 256
                        mm(pb[0:n, oc:oc + 256], aM[0:n, 0:n], vvs[h], True, T == 0, [raM, r_rv[ti]], [rpb])
                        if T != 0:
                            mm(pb[0:n, oc:oc + 256], qkT[:, h, 0:n], Sb[:, h, :], False, True, [r_qkT, r_Sb[h]], [rpb])

                    def segA():
                        St(0); St(1)
                        aT(0); aT(1)
                        aMk(0); aMk(1)
                        aT(2); aT(3)
                        aMk(2); aMk(3)

                    def segB():
                        omm(0); omm(1); omm(2); omm(3)
                        Supd(0); Supd(1)
                        St(2); St(3)
                        Supd(2); Supd(3)
                        if ti + 1 < nt:
                            ret_stage1(ti + 1)

                    def segC():
                        if ret_state["pend"] is not None:
                            ret_state["pend"]()
                            ret_state["pend"] = None
                        for h in range(4):
                            pb, rpb = por[h // 2]
                            oc = (h % 2) * 256
                            P.op("dve", "bn_stats", dict(out=bnst[0:n, h, :], in_=pb[0:n, oc:oc + 256]), reads=[rpb], writes=[r_bn])
                            P.op("dve", "bn_aggr", dict(out=bnmv[0:n, h, :], in_=bnst[0:n, h, :]), reads=[r_bn], writes=[r_bn])
                        P.op("act", "activation", dict(out=hrs[0:n, 0:4], in_=bnmv[0:n, :, 1], func=AF.Sqrt, bias=EPS), reads=[r_bn], writes=[r_bn])
                        P.op("dve", "reciprocal", dict(out=hrs[0:n, 4:8], in_=hrs[0:n, 0:4]), reads=[r_bn], writes=[r_bn])
                        P.op("dve", "scalar_tensor_tensor", dict(out=hrs[0:n, 8:12], in0=bnmv[0:n, :, 0], scalar=-1.0, in1=hrs[0:n, 4:8],
                                                                 op0=ALU.mult, op1=ALU.mult), reads=[r_bn], writes=[r_bn])
                        for h in range(4):
                            pb, rpb = por[h // 2]
                            oc = (h % 2) * 256
                            P.op("dve", "tensor_scalar", dict(out=on16[0:n, h * 256:(h + 1) * 256], in0=pb[0:n, oc:oc + 256],
                                                              scalar1=bnmv[0:n, h, 0:1], scalar2=hrs[0:n, 4 + h:5 + h],
                                                              op0=ALU.subtract, op1=ALU.mult),
                                 reads=[rpb, r_bn], writes=[r_on32])
                        rt_, rrt = rtokR.next()
                        P.op("pool", "tensor_tensor", dict(out=rt_[0:n, :], in0=on16[0:n, :], in1=rgs[0:n, ti, :], op=ALU.mult),
                             reads=[r_on32, r_rgs[ti]], writes=[rrt])

                        def fin_ret():
                            for c in range(8):
                                transp(pT[:, c, 0:n], rt_[0:n, c * 128:(c + 1) * 128], n, [rrt], [r_pT])
                            P.op("act", "activation", dict(out=retT[:, :, c0:c0 + n], in_=pT[:, :, 0:n], func=AF.Copy), reads=[r_pT], writes=[r_ret[ti]])
                        ret_state["pend"] = fin_ret
                    return [segA, segB, segC]

                ret_segs = [lambda: ret_stage1(0)]
                for ti, T in enumerate(grp):
                    ret_segs += make_ret(ti, T)
                gate_rot = Rot([pmmR.items[0], pmmR.items[1]])

                for which, (gT_, rg_) in enumerate([(gaT, r_ga), (grT, r_gr)]):
                    for hf in range(2):
                        W, rW = w_next()
                        for b4 in range(4):
                            blk = hf * 4 + b4
                            def ev(pb, rpb, s0, sl, tis, blk=blk, gT_=gT_, rg_=rg_, which=which):
                                bcol = C("bbr")[:, which * 8 + blk:which * 8 + blk + 1]
                                P.op("act", "activation", dict(out=gT_[:, blk, s0:s0 + sl], in_=pb[:, 0:sl], func=AF.Sigmoid, bias=bcol),
                                     reads=[rpb, r_const], writes=[rg_[blk]])
                            fm_proj(W, rW, 8, b4 * 128, actA, hres, ev, rot=gate_rot)
                            if ret_segs:
                                ret_segs.pop(0)()
                while ret_segs:
                    ret_segs.pop(0)()

                if gcount == 1:
                    dump("hT", actA[:, :, 0:G], [128, 8, G], BF16, r_hT[:nt])
                    dump("fqT", fqT[:, :, 0:G], [128, 8, G], BF16, r_fq)
                    dump("kT", kTc[:, :, 0:G], [128, 4, G], BF16, [r_kT[p][0] for p in range(4)])
                    dump("V", Vc[:, 0:5, :, :], [128, 5, 8, 65], BF16, [r_V[0]])
                    dump("G", Gc[:, 0:5, :], [128, 5, 8], F32, r_G[:5])
                    dump("rq", rq[:, :, :], [128, 5, 512], BF16, r_rq)
                    dump("rk", rk[:, :, :], [128, 5, 512], BF16, r_rk)
                    dump("kdk", kdk[:, :, :], [128, 5, 512], BF16, r_kdk)
                    dump("rv", rv[:, :, :], [128, 5, 1024], BF16, r_rv)
                    dump("rgs", rgs[:, :, :], [128, 5, 1024], BF16, r_rgs)
                    dump("gaT", gaT[:, :, 0:G], [128, 8, G], BF16, r_ga)
                    dump("grT", grT[:, :, 0:G], [128, 8, G], BF16, r_gr)
                qsets = [[0], [1, 2, 3, 4]] if gi == 0 else [[0, 1, 2, 3]]
                for Q in qsets:
                    NQ = sum(ns[t_] for t_ in Q)
                    qc0 = col0[Q[0]]
                    Tq = [grp[t_] for t_ in Q]
                    qcs = [sum(ns[t_] for t_ in Q[:i_]) for i_ in range(len(Q))]
                    for qi, ti in enumerate(Q):
                        T = grp[ti]
                        n = ns[ti]
                        selv = C("selm")[0:n, 0:4] if T == 0 else C("selq")[0:n, qi * 4:qi * 4 + 4]
                        mm(pgref[0:4, :], selv, Gc[0:n, T, :], qi == 0, qi == len(Q) - 1, [r_G[T], r_const], [r_pgref])
                    gv, gr1, gr2 = gpc[0:4, 0:8], gpc[0:4, 8:16], gpc[0:4, 16:24]
                    ghi, gmid, glo = gpb[0:4, 0:8], gpb[0:4, 8:16], gpb[0:4, 16:24]
                    P.op("dve", "tensor_scalar", dict(out=gv, in0=pgref[0:4, :], scalar1=-8.0, scalar2=None, op0=ALU.mult), reads=[r_pgref], writes=[r_gref])
                    P.op("dve", "tensor_copy", dict(out=ghi, in_=gv), reads=[r_gref], writes=[r_gref])
                    P.op("dve", "tensor_tensor", dict(out=gr1, in0=gv, in1=ghi, op=ALU.subtract), reads=[r_gref], writes=[r_gref])
                    P.op("dve", "tensor_copy", dict(out=gmid, in_=gr1), reads=[r_gref], writes=[r_gref])
                    P.op("dve", "tensor_tensor", dict(out=gr2, in0=gr1, in1=gmid, op=ALU.subtract), reads=[r_gref], writes=[r_gref])
                    P.op("dve", "tensor_copy", dict(out=glo, in_=gr2), reads=[r_gref], writes=[r_gref])
                    for pi, gp_ in enumerate([ghi, gmid, glo]):
                        P.op("dve", "tensor_copy", dict(out=cpT[32 * pi:32 * pi + 4, :, :], in_=gp_.unsqueeze(2).to_broadcast([4, 8, 128])),
                             reads=[r_gref], writes=[r_bias])
                    Tmax = Tq[-1]
                    blocks = [(h, kt) for h in range(8) for kt in range(Tmax + 1)]
                    sbanks = [(pSk[0], r_pS[0]), (pSk[1], r_pS[1]), pmmR.items[0], pmmR.items[1]]
                    NB = len(blocks)
                    pts, pend = {}, {}

                    def split(kt):
                        i0 = min(i_ for i_ in range(len(Q)) if Tq[i_] >= kt)
                        cs = qcs[i0]
                        diag = (Tq[i0] == kt)
                        nd = ns[Q[i0]]
                        return i0, cs, diag, nd

                    def S_stage(bi):
                        h, kt = blocks[bi]
                        p, base = h // 2, 64 * (h % 2)
                        nk = tile_n(kt)
                        kc0 = tile_pos(kt)
                        kg = 0 if kt <= 4 else (kt - 1) // 4
                        i0, cs, diag, nd = split(kt)
                        bank, rbank = sbanks[bi % 4]
                        kTv = kTc[:, p, kc0:kc0 + nk]
                        rr = [r_kT[p][kg], r_fq[p]]
                        cpv = cpT[:, h, 0:nk]
                        if diag:
                            mm(bank[0:nk, cs:cs + nd], kTv, fqT[:, h, qc0 + cs:qc0 + cs + nd], True, False, rr, [rbank])
                            mm(bank[0:nk, cs:cs + nd], identb[0:nk, 0:nk], maskT[0:nk, 0:nd], False, False, [r_const], [rbank])
                            mm(bank[0:nk, cs:cs + nd], cpv, indq[:, cs:cs + nd], False, True, [r_const, r_bias], [rbank])
                            if cs + nd < NQ:
                                mm(bank[0:nk, cs + nd:NQ], kTv, fqT[:, h, qc0 + cs + nd:qc0 + NQ], True, False, rr, [rbank])
                                mm(bank[0:nk, cs + nd:NQ], cpv, indq[:, cs + nd:NQ], False, True, [r_const, r_bias], [rbank])
                        else:
                            mm(bank[0:nk, cs:NQ], kTv, fqT[:, h, qc0 + cs:qc0 + NQ], True, False, rr, [rbank])
                            mm(bank[0:nk, cs:NQ], cpv, indq[:, cs:NQ], False, True, [r_const, r_bias], [rbank])
                        pt, rpt = ptR.next()
                        P.op("act", "activation", dict(
                            out=pt[0:nk, cs:NQ], in_=bank[0:nk, cs:NQ], func=AF.Exp, bias=Gc[0:nk, kt, h:h + 1], scale=0.125),
                             reads=[rbank, r_G[kt]], writes=[rpt])
                        pts[bi] = (pt, rpt)

                    def PV_stage(bi, step):
                        h, kt = blocks[bi]
                        p, base = h // 2, 64 * (h % 2)
                        nk = tile_n(kt)
                        kg = 0 if kt <= 4 else (kt - 1) // 4
                        i0, cs, diag, nd = split(kt)
                        po, rpo = pOk[h % 2], r_pO[h % 2]
                        pt, rpt = pts.pop(bi)
                        nkp = 128 if kt == 0 else nk
                        Vv = Vc[0:nkp, kt, h, 0:65]
                        if diag:
                            mm(po[0:65, cs:cs + nd], Vv, pt[0:nkp, cs:cs + nd], kt == 0, True, [r_V[kg], rpt], [rpo])
                            if cs + nd < NQ:
                                mm(po[0:65, cs + nd:NQ], Vv, pt[0:nkp, cs + nd:NQ], kt == 0, False, [r_V[kg], rpt], [rpo])
                        else:
                            mm(po[0:65, cs:NQ], Vv, pt[0:nkp, cs:NQ], kt == 0, False, [r_V[kg], rpt], [rpo])
                        if kt == Tmax:
                            o_, ro_ = o32R.next()
                            P.op("act", "activation", dict(out=o_[0:65, 0:NQ], in_=po[0:65, 0:NQ], func=AF.Copy),
                                 reads=[rpo], writes=[ro_])
                            P.op("dve", "reciprocal", dict(out=o_[64:65, 0:NQ], in_=o_[64:65, 0:NQ]), reads=[ro_], writes=[ro_])

                            def norm2(o_=o_, ro_=ro_, p=p, base=base):
                                pr, rpr = pmmR.items[2]
                                mm(pr[0:64, 0:NQ], C("ones")[:, 0:64], o_[:, 0:NQ], True, True, [ro_, r_const], [rpr])
                                P.op("dve", "tensor_tensor", dict(
                                    out=foxT[base:base + 64, p, qc0:qc0 + NQ], in0=o_[0:64, 0:NQ], in1=pr[0:64, 0:NQ], op=ALU.mult),
                                     reads=[ro_, rpr], writes=[r_fox[t_] for t_ in Q])
                            pend.setdefault(step + min(12, 2 * (Tmax + 1)), []).append(norm2)

                    LA = 3
                    for step in range(NB + LA + 14):
                        for f_ in pend.pop(step, []):
                            f_()
                        if step < NB:
                            S_stage(step)
                        if 0 <= step - LA < NB:
                            PV_stage(step - LA, step)
                    assert not pend and not pts

                if gcount == 1:
                    dump("foxT", foxT[:, :, 0:G], [128, 4, G], BF16, r_fox)
                    dump("retT", retT[:, :, 0:G], [128, 8, G], BF16, r_ret)
                P.op("pool", "memset", dict(ap=stat[:, 61:62], constant=0.0),
                     writes=r_m12 + r_upe + r_uph + r_y0 + r_y2b + r_outb + r_xnD + r_rq + r_rk + r_kdk + r_rv + r_rgs)

                Wfo, rWfo = w_next()
                m12R = Rot([(m12[i], r_m12[i]) for i in range(4)])
                fres = lambda tis: [r_fox[t] for t in tis]
                rres = lambda tis: [r_ret[t] for t in tis]
                for cb_ in range(8):
                    for (s0, sl, tis) in segs:
                        pa1, rp1 = pmmR.next()
                        for c in range(4):
                            mm(pa1[:, 0:sl], Wfo[:, c, cb_ * 128:(cb_ + 1) * 128], foxT[:, c, s0:s0 + sl], c == 0, c == 3, [rWfo] + fres(tis), [rp1])
                        P.op("dve", "tensor_tensor", dict(out=actA[:, cb_, s0:s0 + sl], in0=pa1[:, 0:sl], in1=gaT[:, cb_, s0:s0 + sl], op=ALU.mult),
                             reads=[rp1, r_ga[cb_]], writes=[r_hT[t] for t in tis])
                if ret_state["pend"] is not None:
                    ret_state["pend"]()
                    ret_state["pend"] = None
                for hf in range(2):
                    Wro, rWro = w_next()
                    for cb_ in range(hf * 4, hf * 4 + 4):
                        for (s0, sl, tis) in segs:
                            pa2, rp2 = pmmR.next()
                            for c in range(8):
                                mm(pa2[:, 0:sl], Wro[:, c, (cb_ % 4) * 128:(cb_ % 4 + 1) * 128], retT[:, c, s0:s0 + sl], c == 0, c == 7, [rWro] + rres(tis), [rp2])
                            m2, rm2 = m12R.next()
                            P.op("dve", "tensor_tensor", dict(out=m2[:, 0:sl], in0=pa2[:, 0:sl], in1=grT[:, cb_, s0:s0 + sl], op=ALU.mult),
                                 reads=[rp2, r_gr[cb_]], writes=[rm2])
                            P.op("dve", "tensor_tensor", dict(out=actA[:, cb_, s0:s0 + sl], in0=m2[:, 0:sl], in1=actA[:, cb_, s0:s0 + sl], op=ALU.add),
                                 reads=[rm2] + [r_hT[t] for t in tis], writes=[r_hT[t] for t in tis])
                for hf in range(2):
                    W, rW = w_next()
                    for ti, T in enumerate(grp):
                        def ev(pb, rpb, n, ti=ti, hf=hf):
                            P.op("dve", "tensor_tensor", dict(out=xres[0:n, ti, hf * 512:(hf + 1) * 512], in0=pb[0:n, 0:512],
                                                                  in1=xres[0:n, ti, hf * 512:(hf + 1) * 512], op=ALU.add),
                                 reads=[rpb, r_x[ti]], writes=[r_x[ti]])
                        tm_proj(W, rW, 8, 512, actA, r_hT[ti], ti, ev)

                if gcount == 1:
                    dump("x1", xres[:, :, :], [128, 5, D], F32, r_x)
                P.op("pool", "memset", dict(ap=stat[:, 62:63], constant=0.0), writes=r_act + r_ga + r_gr + r_ret)

                for ti, T in enumerate(grp):
                    rmsnorm_T(ti, ns[ti], col0[ti], "gffn", xnDR)
                cw3 = C("cw").rearrange("p (b k) -> p b k", k=3)
                cbv = C("cbias")
                upR = Rot([(upe[i], r_upe[i], r_uph[i]) for i in range(3)])
                y0R = Rot([(y0b[i], r_y0[i]) for i in range(4)])
                y2R = Rot([(y2b[i], r_y2b[i]) for i in range(3)])
                W, rW = None, None
                fin = {}
                SKEW = 2
                for b in range(44 + SKEW):
                    if b < 44:
                        if b % 4 == 0:
                            W, rW = w_next()
                        f = b % 22
                        up_, rup, ruh = upR.next()
                        y0_, ry0 = y0R.next()
                        P.op("dve", "tensor_copy", dict(out=up_[:, 0:2], in_=halo[:, b, :]), reads=[r_halo[b]], writes=[ruh])
                        for (s0, sl, tis) in segs:
                            pb, rpb = pmmR.next()
                            for c in range(8):
                                mm(pb[:, 0:sl], W[:, c, (b % 4) * 128:(b % 4 + 1) * 128], actA[:, c, s0:s0 + sl], c == 0, c == 7, [rW] + hres(tis), [rpb])
                            P.op("act", "activation", dict(out=up_[:, 2 + s0:2 + s0 + sl], in_=pb[:, 0:sl], func=AF.Copy),
                                 reads=[rpb], writes=[rup])
                            P.op("act", "activation", dict(out=y0_[:, s0:s0 + sl], in_=pb[:, 0:sl], func=AF.Identity,
                                                           scale=cw3[:, b, 2:3], bias=cbv[:, b:b + 1]),
                                 reads=[rpb, r_const], writes=[ry0])
                        P.op("dve", "scalar_tensor_tensor", dict(out=y0_[:, 0:G], in0=up_[:, 1:1 + G], scalar=cw3[:, b, 1:2], in1=y0_[:, 0:G],
                                                                 op0=ALU.mult, op1=ALU.add), reads=[rup, ruh, ry0, r_const], writes=[ry0])
                        if b < 22:
                            yo_, ryo = y0_, ry0
                        else:
                            yo_, ryo = y2R.next()
                        P.op("dve", "scalar_tensor_tensor", dict(out=yo_[:, 0:G], in0=up_[:, 0:G], scalar=cw3[:, b, 0:1], in1=y0_[:, 0:G],
                                                                 op0=ALU.mult, op1=ALU.add), reads=[rup, ruh, ry0, r_const], writes=[ryo] if b >= 22 else [ry0])
                        P.op("dve", "tensor_copy", dict(out=halo[:, b, :], in_=up_[:, G:G + 2]), reads=[rup], writes=[r_halo[b]])
                        fin[b] = (yo_, ryo, f)
                    bb = b - SKEW
                    if bb >= 0:
                        y0_, ry0, f = fin.pop(bb)
                        if bb < 22:
                            P.op("act", "activation", dict(out=actT[:, f, 0:G], in_=y0_[:, 0:G], func=AF.Silu), reads=[ry0], writes=[r_act[f]])
                        else:
                            P.op("dve", "tensor_tensor", dict(out=actT[:, f, 0:G], in0=y0_[:, 0:G], in1=actT[:, f, 0:G], op=ALU.mult),
                                 reads=[ry0, r_act[f]], writes=[r_act[f]])
                P.op("pool", "memset", dict(ap=stat[:, 63:64], constant=0.0), writes=r_m12 + r_upe + r_uph + r_y0 + r_outb)
                gidx = all_groups.index((seq, gi))
                nxt = all_groups[gidx + 1] if gidx + 1 < len(all_groups) else None
                if dbg is not None and gcount >= dbg.get("ngroups", 99):
                    nxt = None

                def final_tile(ti, T):
                    n = ns[ti]
                    if T != 0:
                        st_, rst = statR.next()
                        P.op("act", "activation", dict(out=junk[0:n, :], in_=xres[0:n, ti, :], func=AF.Square, accum_out=st_[0:n, 0:1]),
                             reads=[r_x[ti]], writes=[rst, r_on32])
                        P.op("act", "activation", dict(out=st_[0:n, 1:2], in_=st_[0:n, 0:1], func=AF.Sqrt, scale=1.0 / D, bias=EPS),
                             reads=[rst], writes=[rst])
                        P.op("dve", "reciprocal", dict(out=st_[0:n, 2:3], in_=st_[0:n, 1:2]), reads=[rst], writes=[rst])
                        ob, rob = outb[ti], r_outb[ti]
                        P.op("dve", "scalar_tensor_tensor", dict(out=ob[0:n, :], in0=xres[0:n, ti, :], scalar=st_[0:n, 2:3], in1=gfb[0:n, :],
                                                                 op0=ALU.mult, op1=ALU.mult), reads=[r_x[ti], rst, r_const], writes=[rob])
                        P.dma("sp", out_d[seq, 128 * (T - 1):128 * T, :], ob[0:n, :], reads=[rob], writes=[R("outdram")], key=("o", ti))
                    if nxt is not None:
                        ngrp = GROUPS[nxt[1]]
                        if ti < len(ngrp):
                            xload(nxt[0], ngrp[ti], ti)
                            prefetched.add(ti)

                for q8 in range(8):
                    W, rW = w_next()
                    for ti, T in enumerate(grp):
                        n = ns[ti]
                        pb, rpb = pmmR.next()
                        for f in range(22):
                            mm(pb[0:n, 0:128], actT[:, f, col0[ti]:col0[ti] + n], W[:, f, 0:128], f == 0, f == 21, [rW, r_act[f]], [rpb])
                        P.op("dve", "tensor_tensor", dict(out=xres[0:n, ti, q8 * 128:(q8 + 1) * 128], in0=pb[0:n, 0:128],
                                                          in1=xres[0:n, ti, q8 * 128:(q8 + 1) * 128], op=ALU.add),
                             reads=[rpb, r_x[ti]], writes=[r_x[ti]])
                        if q8 == 7:
                            if gcount == 1 and ti == nt - 1:
                                dump("x2", xres[:, :, :], [128, 5, D], F32, r_x)
                            final_tile(ti, T)
        P.emit()
    nc._n_ops = len(P.ops)
    return nc


def _prep_inputs(x, meta, norm_mix_g, w_in, b_forget, b_branch, w_fox_o, w_ret_o, w_out,
                 norm_ffn_g, w_up, conv_w, conv_b, w_down, norm_f_g):
    f = lambda a: np.ascontiguousarray(np.asarray(a, dtype=np.float32))
    cfh, cbh = make_consts()
    cfa = np.zeros((128, CF_TOT), np.float32)
    for k in ["tri", "sel0", "selq", "selm", "sel15", "sel127", "ones", "qds", "kd128", "kd16", "M128", "M16"]:
        cfa[:, CF_OFF[k]:CF_OFF[k] + CF_W[k]] = cfh[k]
    cfa[:, CF_OFF["gmix"]:CF_OFF["gmix"] + 8] = f(norm_mix_g)[0].reshape(8, 128).T
    cfa[:, CF_OFF["gffn"]:CF_OFF["gffn"] + 8] = f(norm_ffn_g)[0].reshape(8, 128).T
    cfa[:, CF_OFF["bfb"]:CF_OFF["bfb"] + 8] = np.broadcast_to(f(b_forget)[0][None, :], (128, 8))
    cfa[:, CF_OFF["bbr"]:CF_OFF["bbr"] + 16] = f(b_branch)[0].reshape(16, 128).T
    cw = f(conv_w)[0]
    cfa[:, CF_OFF["cw"]:CF_OFF["cw"] + 132] = cw.reshape(3, 44, 128).transpose(2, 1, 0).reshape(128, 132)
    cfa[:, CF_OFF["cbias"]:CF_OFF["cbias"] + 44] = f(conv_b)[0].reshape(44, 128).T
    cba = np.concatenate([cbh["ident"], cbh["maskT"], cbh["ind"]], axis=1)
    common = dict(meta=f(meta), w_in=f(w_in)[0], w_fox_o=f(w_fox_o)[0], w_ret_o=f(w_ret_o)[0], w_out=f(w_out)[0],
                  w_up=f(w_up)[0], w_down=f(w_down)[0], cf=cfa, cb=np.ascontiguousarray(cba),
                  gfb=np.ascontiguousarray(np.broadcast_to(f(norm_f_g)[None, :], (128, D))),
                  cos=cfh["cos"], sin=cfh["sin"])
    xs = f(x)
    in_maps = []
    for c in range(NCORES):
        m = dict(common)
        m["x"] = np.ascontiguousarray(xs[c * SPC:(c + 1) * SPC])
        in_maps.append(m)
    return in_maps


def kernel(**inputs):
    in_maps = _prep_inputs(**inputs)
    nc = build_nc()
    res = run_bass_kernel_spmd(nc, in_maps, core_ids=list(range(NCORES)))
    out = np.concatenate([np.asarray(r["out"]) for r in res.results], axis=0)
    return out.astype(np.float32)
```

```python
import contextlib
import numpy as np
import ml_dtypes
import concourse.bass as bass
import concourse.mybir as mybir
from concourse.bass_utils import run_bass_kernel_spmd

F32 = mybir.dt.float32
BF16 = mybir.dt.bfloat16
AF = mybir.ActivationFunctionType
ALU = mybir.AluOpType

D = 1024
BATCH = 16
SEQ = 2048
NMETA = 16
NCORES = 8
SPC = BATCH // NCORES
DIN = 6664
DFF = 2816
EPS = 1e-6
NT = 17
GROUPS = [[0, 1, 2, 3, 4], [5, 6, 7, 8], [9, 10, 11, 12], [13, 14, 15, 16]]
GMAX = 528
NEGM = -30000.0
WSLOT = 4096
NSLOT = 3


def tile_n(T):
    return NMETA if T == 0 else 128


def tile_pos(T):
    return 0 if T == 0 else NMETA + 128 * (T - 1)


class Res:
    __slots__ = ("name", "last_w", "readers", "excl")

    def __init__(self, name, excl=False):
        self.name = name
        self.last_w = None
        self.readers = []
        self.excl = excl


class Op:
    __slots__ = ("eng", "fn", "deps", "sig", "key", "val", "is_dma", "idx")


class Prog:
    ENG = ("pe", "act", "dve", "pool", "sp")

    def __init__(self, nc):
        self.nc = nc
        self.ops = []
        self.same_engine_sync = {"pe": False, "act": True, "dve": True, "pool": True, "sp": False}

    def res(self, name):
        return Res(name)

    def op(self, eng, meth, kw, reads=(), writes=(), dma_key=None):
        o = Op()
        o.eng = eng
        o.fn = (meth, kw)
        o.sig = False
        o.is_dma = dma_key is not None
        o.key = dma_key if o.is_dma else eng
        o.val = None
        o.idx = len(self.ops)
        deps = set()
        for r in reads:
            if r.last_w is not None:
                deps.add(r.last_w)
            if r.excl:
                for rd in r.readers:
                    if self.ops[rd].eng != eng:
                        deps.add(rd)
        for r in writes:
            if r.last_w is not None:
                deps.add(r.last_w)
            deps.update(r.readers)
        deps.discard(o.idx)
        o.deps = deps
        for r in reads:
            r.readers.append(o.idx)
        for r in writes:
            r.last_w = o.idx
            r.readers = []
        self.ops.append(o)
        return o

    def dma(self, eng, out_ap, in_ap, reads=(), writes=(), key=None):
        return self.op(eng, "dma_start", dict(out=out_ap, in_=in_ap), reads, writes, dma_key=key)

    def emit(self, final_wait_eng="sp"):
        nc = self.nc
        ops = self.ops
        for o in ops:
            need = set()
            for d in o.deps:
                p = ops[d]
                if (not p.is_dma) and p.eng == o.eng and not self.same_engine_sync[o.eng]:
                    continue
                need.add(d)
            o.deps = need
            for d in need:
                ops[d].sig = True
        for o in ops:
            if o.is_dma:
                o.sig = True
        last_of_eng = {}
        for o in ops:
            if not o.is_dma:
                last_of_eng[o.eng] = o
        for o in last_of_eng.values():
            o.sig = True
        counts = {}
        for o in ops:
            if o.sig:
                inc = 16 if o.is_dma else 1
                counts[o.key] = counts.get(o.key, 0) + inc
                o.val = counts[o.key]
        keys = sorted(counts.keys(), key=str)
        with contextlib.ExitStack() as st:
            sems = {k: st.enter_context(nc.semaphore("s_%s" % str(k).replace(" ", ""))) for k in keys}
            block = st.enter_context(nc.Block())
            per_eng = {e: [o for o in ops if o.eng == e] for e in self.ENG}

            def run_engine(ename, e):
                seen = {}
                for o in per_eng[ename]:
                    waits = {}
                    for d in o.deps:
                        p = ops[d]
                        if waits.get(p.key, 0) < p.val:
                            waits[p.key] = p.val
                    for k, v in waits.items():
                        if seen.get(k, 0) >= v:
                            continue
                        e.wait_ge(sems[k], v)
                        seen[k] = v
                    ins = getattr(e, o.fn[0])(**o.fn[1])
                    if o.sig:
                        ins.then_inc(sems[o.key], 16 if o.is_dma else 1)
                if ename == final_wait_eng:
                    for k in keys:
                        if k != ename:
                            e.wait_ge(sems[k], counts[k])

            @block.tensor
            def _(e):
                run_engine("pe", e)

            @block.scalar
            def _(e):
                run_engine("act", e)

            @block.vector
            def _(e):
                run_engine("dve", e)

            @block.gpsimd
            def _(e):
                run_engine("pool", e)

            @block.sync
            def _(e):
                run_engine("sp", e)


class Rot:
    def __init__(self, items):
        self.items = items
        self.i = 0

    def next(self):
        it = self.items[self.i % len(self.items)]
        self.i += 1
        return it


def make_consts():
    cf = {}
    tri = np.triu(np.ones((128, 128), np.float32))
    cf["tri"] = tri
    sel0 = np.zeros((128, 128), np.float32); sel0[0, :] = 1
    sel15 = np.zeros((128, 128), np.float32); sel15[15, :] = 1
    sel127 = np.zeros((128, 128), np.float32); sel127[127, :] = 1
    cf["sel0"], cf["sel15"], cf["sel127"] = sel0, sel15, sel127
    selq = np.zeros((128, 4, 4), np.float32)
    for qi in range(4):
        selq[64, qi, qi] = 1
    cf["selq"] = selq.reshape(128, 16)
    selm = np.zeros((128, 4), np.float32)
    selm[8, 0] = 1
    cf["selm"] = selm
    ones = np.zeros((128, 64), np.float32); ones[64, :] = 1
    cf["ones"] = ones
    hh = np.arange(4, dtype=np.float64)
    gam = 1.0 - 2.0 ** (-5.0 - hh)
    scale = 128.0 ** -0.5
    idx = np.arange(128, dtype=np.float64)
    qds = scale * gam[None, :] ** (idx[:, None] + 1.0)
    cf["qds"] = qds.astype(np.float32)
    cf["kd128"] = (gam[None, :] ** (127.0 - idx[:, None])).astype(np.float32)
    kd16 = np.ones((128, 4)); kd16[:16] = gam[None, :] ** (15.0 - idx[:16, None])
    cf["kd16"] = kd16.astype(np.float32)
    j = idx[:, None]; i = idx[None, :]
    M = np.zeros((128, 4, 128))
    allowed = (np.floor(j / 64) <= np.floor(i / 64))
    for h in range(4):
        M[:, h, :] = np.where(allowed, gam[h] ** (np.abs(i - j) - (i + 1.0)), 0.0)
    cf["M128"] = M.reshape(128, 512).astype(np.float32)
    M16 = np.zeros((128, 4, 16))
    for h in range(4):
        M16[:16, h, :] = gam[h] ** (np.abs(i[:, :16] - j[:16]) - (i[:, :16] + 1.0))
    cf["M16"] = M16.reshape(128, 64).astype(np.float32)
    cf["cd"] = [float(g ** 128.0) for g in gam]
    inv = 10000.0 ** (-np.arange(0, 128, 2, dtype=np.float64) / 128.0)
    cos = np.zeros((128, NT, 64)); sin = np.zeros((128, NT, 64))
    for T in range(NT):
        n = tile_n(T)
        pos = (tile_pos(T) + np.arange(n)).astype(np.float32).astype(np.float64)
        ang = (pos[:, None].astype(np.float32) * inv[None, :].astype(np.float32)).astype(np.float32)
        cos[:n, T] = np.cos(ang.astype(np.float64)); sin[:n, T] = np.sin(ang.astype(np.float64))
    cf["cos"] = cos.reshape(128, NT * 64).astype(np.float32)
    cf["sin"] = sin.reshape(128, NT * 64).astype(np.float32)
    cb = {}
    cb["ident"] = np.eye(128, dtype=np.float32).astype(ml_dtypes.bfloat16)
    maskT = np.where(np.arange(128)[:, None] <= np.arange(128)[None, :], 0.0, NEGM).astype(np.float32)
    cb["maskT"] = maskT.astype(ml_dtypes.bfloat16)
    ind = np.zeros((128, 512), np.float32)
    for r in range(4):
        for base in (0, 32, 64):
            ind[base + r, r * 128:(r + 1) * 128] = 1
    cb["ind"] = ind.astype(ml_dtypes.bfloat16)
    return cf, cb


CF_KEYS = ["tri", "sel0", "selq", "selm", "sel15", "sel127", "ones", "qds", "kd128", "kd16", "M128", "M16",
           "gmix", "gffn", "bfb", "bbr", "cw", "cbias"]
CF_W = {"tri": 128, "sel0": 128, "selq": 16, "selm": 4, "sel15": 128, "sel127": 128, "ones": 64, "qds": 4, "kd128": 4, "kd16": 4,
        "M128": 512, "M16": 64, "gmix": 8, "gffn": 8, "bfb": 8, "bbr": 16, "cw": 132, "cbias": 44}
CF_OFF = {}
_o = 0
for _k in CF_KEYS:
    CF_OFF[_k] = _o
    _o += CF_W[_k]
CF_TOT = _o


def build_nc(dbg=None):
    nc = bass.Bass("TRN2", target_bir_lowering=False)
    cfh, _ = make_consts()
    cd = cfh["cd"]

    def din(name, shape, dt=F32):
        return nc.dram_tensor(name, shape, dt, kind="ExternalInput").ap()

    x_d = din("x", [SPC, SEQ, D])
    meta_d = din("meta", [NMETA, D])
    w_in_d = din("w_in", [D, DIN])
    w_fo_d = din("w_fox_o", [512, D])
    w_ro_d = din("w_ret_o", [D, D])
    w_out_d = din("w_out", [D, D])
    w_up_d = din("w_up", [D, 2 * DFF])
    w_dn_d = din("w_down", [DFF, D])
    cf_d = din("cf", [128, CF_TOT])
    cb_d = din("cb", [128, 768], BF16)
    gf_d = din("gfb", [128, D])
    cos_d = din("cos", [128, NT * 64])
    sin_d = din("sin", [128, NT * 64])
    out_d = nc.dram_tensor("out", [SPC, SEQ, D], F32, kind="ExternalOutput").ap()

    P = Prog(nc)
    with contextlib.ExitStack() as st:
        def sb(name, shape, dt):
            return st.enter_context(nc.sbuf_tensor(name, shape, dt))

        def ps(name, shape, dt):
            return st.enter_context(nc.psum_tensor(name, shape, dt))

        cf = sb("cf_s", [128, CF_TOT], F32)
        cbt = sb("cbt", [128, 768], BF16)
        gfb = sb("gfb_s", [128, D], F32)
        cosg = sb("cosg", [128, 5 * 64], F32)
        sing = sb("sing", [128, 5 * 64], F32)
        xres = sb("xres", [128, 5, D], F32)
        actA = sb("actA", [128, 8, GMAX], BF16)
        fqT = sb("fqT", [128, 8, GMAX], BF16)
        kTc = sb("kTc", [128, 4, SEQ + NMETA], BF16)
        Vc = sb("Vc", [128, NT, 8, 65], BF16)
        Gc = sb("Gc", [128, NT, 8], F32)
        cpT = sb("cpT", [128, 8, 128], BF16)
        gpc = sb("gpc", [128, 64], F32)
        gpb = sb("gpb", [128, 32], BF16)
        big = sb("big", [128, 3 * 8 * GMAX], BF16)
        ar1 = sb("ar1", [128, 8960], F32)
        foxT = sb("foxT", [128, 4, GMAX], BF16)
        S32 = sb("S32", [128, 4, 256], F32)
        Sb = sb("Sb", [128, 4, 256], BF16)
        wsl = [sb("wsl%d" % i, [128, WSLOT], BF16) for i in range(NSLOT)]
        halo = sb("halo", [128, 44, 2], F32)
        stat = sb("stat", [128, 64], F32)
        xn2 = [sb("xn%d" % i, [128, D], BF16) for i in range(2)]
        r32 = [sb("r32_%d" % i, [128, 512], F32) for i in range(2)]
        tq = [sb("tq%d" % i, [128, 512], F32) for i in range(2)]
        ptb = [sb("ptb%d" % i, [128, 512], BF16) for i in range(4)]
        o32 = [sb("o32_%d" % i, [128, 512], F32) for i in range(2)]
        qkT = sb("qkT", [128, 8, 128], BF16)
        aMb = [sb("aM%d" % i, [128, 128], BF16) for i in range(4)]
        on16 = sb("on16", [128, D], BF16)
        junk = on16
        rtok = xn2
        lz = sb("lz", [128, 16], F32)
        lzt = sb("lzt", [128, 5, 16], F32)
        gref = sb("gref", [128, 8], F32)
        bnst = sb("bnst", [128, 4, 6], F32)
        bnmv = sb("bnmv", [128, 4, 2], F32)
        hrs = sb("hrs", [128, 16], F32)

        gaT = big[:, 0:8 * GMAX].rearrange("p (c n) -> p c n", c=8)
        grT = big[:, 8 * GMAX:16 * GMAX].rearrange("p (c n) -> p c n", c=8)
        retT = big[:, 16 * GMAX:24 * GMAX].rearrange("p (c n) -> p c n", c=8)
        actT = big[:, 0:22 * GMAX].rearrange("p (c n) -> p c n", c=22)
        ar1b = ar1.bitcast(BF16)
        rq = ar1b[:, 0:2560].rearrange("p (t n) -> p t n", t=5)
        rk = ar1b[:, 2560:5120].rearrange("p (t n) -> p t n", t=5)
        kdk = ar1b[:, 5120:7680].rearrange("p (t n) -> p t n", t=5)
        rv = ar1b[:, 7680:12800].rearrange("p (t n) -> p t n", t=5)
        rgs = ar1b[:, 12800:17920].rearrange("p (t n) -> p t n", t=5)
        m12 = [ar1[:, i * 512:(i + 1) * 512] for i in range(4)]
        upe = [ar1[:, 2048 + i * 544: 2048 + (i + 1) * 544] for i in range(3)]
        y0b = [ar1[:, 3680 + i * 544: 3680 + (i + 1) * 544] for i in range(4)]
        outb = [ar1[:, i * 1024:(i + 1) * 1024] for i in range(5)]
        xnD = [ar1b[:, 15808 + i * 1024: 15808 + (i + 1) * 1024] for i in range(2)]
        y2b = [ar1b[:, 11712 + i * 1088: 11712 + (i + 1) * 1088] for i in range(3)]

        pmm = [ps("pmm%d" % i, [128, 512], F32) for i in range(3)]
        pT = ps("pT", [128, 8, 128], BF16)
        pSk = [ps("pS%d" % i, [128, 512], F32) for i in range(2)]
        pOk = [ps("pO%d" % i, [128, 512], F32) for i in range(2)]

        R = P.res
        r_const = R("const")
        r_cs = R("cossin")
        r_x = [R("x%d" % i) for i in range(5)]
        r_hT = [R("hT%d" % i) for i in range(5)]
        r_fq = [R("fq%d" % p) for p in range(4)]
        r_kT = [[R("kT%d_%d" % (p, g)) for g in range(4)] for p in range(4)]
        r_V = [R("V%d" % g) for g in range(4)]
        r_G = [R("G%d" % T) for T in range(NT)]
        r_bias = R("biasJ")
        r_ga = [R("ga%d" % c) for c in range(8)]
        r_gr = [R("gr%d" % c) for c in range(8)]
        r_ret = [R("retT%d" % i) for i in range(5)]
        r_act = [R("actT%d" % f) for f in range(22)]
        r_rq = [R("rq%d" % i) for i in range(5)]
        r_rk = [R("rk%d" % i) for i in range(5)]
        r_kdk = [R("kdk%d" % i) for i in range(5)]
        r_rv = [R("rv%d" % i) for i in range(5)]
        r_rgs = [R("rgs%d" % i) for i in range(5)]
        r_fox = [R("fox%d" % i) for i in range(5)]
        r_S32 = [R("S32_%d" % h) for h in range(4)]
        r_Sb = [R("Sb%d" % h) for h in range(4)]
        r_w = [R("w%d" % i) for i in range(NSLOT)]
        r_halo = [R("halo%d" % b) for b in range(44)]
        r_stat = [R("stat%d" % i) for i in range(8)]
        r_m12 = [R("m12_%d" % i) for i in range(4)]
        r_upe = [R("upe%d" % i) for i in range(3)]
        r_uph = [R("uph%d" % i) for i in range(3)]
        r_y0 = [R("y0_%d" % i) for i in range(4)]
        r_y2b = [R("y2b%d" % i) for i in range(3)]
        r_outb = [R("outb%d" % i) for i in range(5)]
        r_out = R("outdram")
        r_xnD = [R("xnD%d" % i) for i in range(2)]

        xnR = Rot([(xn2[i], R("xn%d" % i)) for i in range(2)])
        xnDR = Rot([(xnD[i], r_xnD[i]) for i in range(2)])
        r32R = Rot([(r32[i], R("r32_%d" % i)) for i in range(2)])
        tqR = Rot([(tq[i], R("tq%d" % i)) for i in range(2)])
        ptR = Rot([(ptb[i], R("pt%d" % i)) for i in range(4)])
        o32R = Rot([(o32[i], R("o32_%d" % i)) for i in range(2)])
        aMR = Rot([(aMb[i], R("aM%d" % i)) for i in range(4)])
        r_qkT, r_on32, r_lz, r_gref, r_bn = R("qkT"), R("on32"), R("lz"), R("gref"), R("bn")
        rtokR = Rot([(rtok[i], xnR.items[i][1]) for i in range(2)])
        XR = lambda nm: Res(nm, excl=True)
        pmmR = Rot([(pmm[i], XR("pmm%d" % i)) for i in range(3)])
        r_pT = XR("pT")
        r_pS = [XR("pS%d" % i) for i in range(2)]
        r_pO = [XR("pO%d" % i) for i in range(2)]
        pcum = pSk[0][:, 0:8]
        pgref = pSk[0][:, 8:16]
        r_pcum = r_pS[0]
        r_pgref = r_pS[0]
        pRA = pSk[0][:, 128:256]
        r_pRA = r_pS[0]
        pSt = pSk[1][:, 0:256]
        r_pSt = r_pS[1]
        statR = Rot([(stat[:, 8 * i:8 * i + 8], r_stat[i]) for i in range(8)])

        def C(key, lo=0, hi=None):
            w = CF_W[key]
            hi = w if hi is None else hi
            return cf[:, CF_OFF[key] + lo:CF_OFF[key] + hi]

        identb = cbt[:, 0:128]
        maskT = cbt[:, 128:256]
        indq = cbt[:, 256:768]

        P.dma("sp", cf[:], cf_d, writes=[r_const], key="c0")
        P.dma("sp", cbt[:], cb_d, writes=[r_const], key="c1")
        P.dma("sp", gfb[:], gf_d, writes=[r_const], key="c2")
        P.op("dve", "memset", dict(ap=Vc[:], constant=1.0), writes=r_V)
        P.op("dve", "memset", dict(ap=Vc[:, 0, :, :], constant=0.0), writes=r_V)
        P.op("dve", "memset", dict(ap=Vc[0:16, 0, :, 64:65], constant=1.0), writes=r_V)
        for pi_ in range(4):
            P.op("dve", "memset", dict(ap=ptb[pi_][:], constant=0.0), writes=[ptR.items[pi_][1]])
        P.op("dve", "memset", dict(ap=cpT[:], constant=0.0), writes=[r_bias])
        P.op("dve", "memset", dict(ap=fqT[:], constant=0.0), writes=r_fq)
        for o_i in range(2):
            P.op("dve", "memset", dict(ap=o32[o_i][:], constant=0.0), writes=[o32R.items[o_i][1]])

        def wdesc_list():
            L_ = []
            secs = [0, 512, 1024, 1536, 1544, 2056, 2568, 3080, 3592, 4104, 4616, 5128, 5640, 6152, 6664]
            for i in [2, 0, 1, 3, 4, 5, 6, 7, 8, 9, 10, 11, 12, 13]:
                c0, c1 = secs[i], secs[i + 1]
                L_.append((w_in_d[:, c0:c1].rearrange("(c p) n -> p c n", p=128), 8, c1 - c0))
            L_.append((w_fo_d.rearrange("(c p) n -> p c n", p=128), 4, 1024))
            for hf in range(2):
                L_.append((w_ro_d[:, hf * 512:(hf + 1) * 512].rearrange("(c p) n -> p c n", p=128), 8, 512))
            for hf in range(2):
                L_.append((w_out_d[:, hf * 512:(hf + 1) * 512].rearrange("(c p) n -> p c n", p=128), 8, 512))
            for i in range(11):
                L_.append(([w_up_d[:, i * 256:(i + 1) * 256].rearrange("(c p) n -> p c n", p=128),
                            w_up_d[:, DFF + i * 256:DFF + (i + 1) * 256].rearrange("(c p) n -> p c n", p=128)], 8, 512))
            for i in range(8):
                L_.append((w_dn_d[:, i * 128:(i + 1) * 128].rearrange("(c p) n -> p c n", p=128), 22, 128))
            return L_

        wdesc = wdesc_list()
        NCH = len(wdesc)
        NGRP = SPC * 4
        total_chunks = NCH * NGRP
        wstate = {"issued": 0, "cur": 0}

        wscr = nc.dram_tensor("wscr", [NCH, 128, WSLOT], BF16, kind="Internal").ap()
        r_scr = [R("scr%d" % i) for i in range(NCH)]

        def w_issue_upto(k):
            while wstate["issued"] <= min(k, total_chunks - 1):
                i = wstate["issued"]
                ci = i % NCH
                src, kc, ncol = wdesc[ci]
                s = i % NSLOT
                if i < NCH:
                    dst = wsl[s][:, 0:kc * ncol].rearrange("p (c n) -> p c n", c=kc)
                    if isinstance(src, list):
                        hw_ = ncol // len(src)
                        for si_, src_ in enumerate(src):
                            P.dma("pool", dst[:, :, si_ * hw_:(si_ + 1) * hw_], src_, writes=[r_w[s]], key=("w", s))
                    else:
                        P.dma("pool", dst, src, writes=[r_w[s]], key=("w", s))
                    if NGRP > 1:
                        P.dma("sp", wscr[ci, :, 0:kc * ncol], wsl[s][:, 0:kc * ncol], reads=[r_w[s]], writes=[r_scr[ci]], key=("ws", s))
                else:
                    P.dma("pool", wsl[s][:, 0:kc * ncol], wscr[ci, :, 0:kc * ncol], reads=[r_scr[ci]], writes=[r_w[s]], key=("w", s))
                wstate["issued"] += 1

        def w_next():
            i = wstate["cur"]
            w_issue_upto(i + NSLOT - 1)
            src, kc, ncol = wdesc[i % NCH]
            s = i % NSLOT
            wstate["cur"] += 1
            return wsl[s][:, 0:kc * ncol].rearrange("p (c n) -> p c n", c=kc), r_w[s]

        def mm(out, lhsT, rhs, start, stop, reads, writes):
            P.op("pe", "matmul", dict(out=out, lhsT=lhsT, rhs=rhs, start=start, stop=stop), reads=reads, writes=writes)

        def transp(out, in_, n, reads, writes):
            P.op("pe", "transpose", dict(out=out, in_=in_, identity=identb[0:n, 0:n]),
                 reads=list(reads) + [r_const], writes=writes)

        def rmsnorm_T(ti, n, c0, gkey, xrot):
            st_, rst = statR.next()
            P.op("act", "activation", dict(out=junk[0:n, :], in_=xres[0:n, ti, :], func=AF.Square,
                                               accum_out=st_[0:n, 0:1]), reads=[r_x[ti]], writes=[rst, r_on32])
            P.op("act", "activation", dict(out=st_[0:n, 1:2], in_=st_[0:n, 0:1], func=AF.Sqrt, scale=1.0 / D, bias=EPS),
                 reads=[rst], writes=[rst])
            P.op("dve", "reciprocal", dict(out=st_[0:n, 2:3], in_=st_[0:n, 1:2]), reads=[rst], writes=[rst])
            xn, rxn = xrot.next()
            P.op("dve", "tensor_scalar", dict(out=xn[0:n, :], in0=xres[0:n, ti, :], scalar1=st_[0:n, 2:3], scalar2=None,
                                                  op0=ALU.mult), reads=[r_x[ti], rst], writes=[rxn])
            for c in range(8):
                transp(pT[:, c, 0:n], xn[0:n, c * 128:(c + 1) * 128], n, [rxn], [r_pT])
            g = C(gkey)
            P.op("dve", "tensor_tensor", dict(out=actA[:, :, c0:c0 + n], in0=pT[:, :, 0:n],
                                                  in1=g.unsqueeze(2).to_broadcast([128, 8, n]), op=ALU.mult),
                 reads=[r_pT, r_const], writes=[r_hT[ti]])

        dbg_outs = []

        def dump(name, ap, shape, dt, reads):
            if dbg is None or name not in dbg:
                return
            d = nc.dram_tensor("dbg_" + name, shape, dt, kind="ExternalOutput").ap()
            P.dma("sp", d, ap, reads=reads, writes=[R("dbg" + name)], key="dbg_" + name)
            dbg_outs.append(name)

        gcount = 0
        all_groups = [(sq_, g_) for sq_ in range(SPC) for g_ in range(len(GROUPS))]
        prefetched = set()

        def xload(seq_, T_, slot):
            n_ = tile_n(T_)
            src = meta_d if T_ == 0 else x_d[seq_, 128 * (T_ - 1):128 * T_, :]
            P.dma("sp", xres[0:n_, slot, :], src, writes=[r_x[slot]], key=("x", slot))

        for seq in range(SPC):
            P.op("dve", "memset", dict(ap=halo[:], constant=0.0), writes=r_halo)
            for gi, grp in enumerate(GROUPS):
                if dbg is not None and gcount >= dbg.get("ngroups", 99):
                    break
                gcount += 1
                nt = len(grp)
                ns = [tile_n(T) for T in grp]
                col0 = [sum(ns[:i]) for i in range(nt)]
                G = sum(ns)
                if gi == 0:
                    segs = [(0, NMETA, [0]), (NMETA, G - NMETA, [1, 2, 3, 4])]
                else:
                    segs = [(0, G, [0, 1, 2, 3])]
                kv0 = tile_pos(grp[0])

                P.op("pool", "memset", dict(ap=stat[:, 60:61], constant=0.0),
                     writes=r_m12 + r_upe + r_uph + r_y0 + r_y2b + r_outb + r_xnD + r_rq + r_rk + r_kdk + r_rv + r_rgs + r_act + r_ga + r_gr + r_ret)

                T0 = grp[0]
                P.dma("sp", cosg[:, 0:nt * 64], cos_d[:, T0 * 64:(T0 + nt) * 64], writes=[r_cs], key="cs0")
                P.dma("sp", sing[:, 0:nt * 64], sin_d[:, T0 * 64:(T0 + nt) * 64], writes=[r_cs], key="cs1")
                for ti, T in enumerate(grp):
                    if ti not in prefetched:
                        xload(seq, T, ti)
                prefetched.clear()
                for ti, T in enumerate(grp):
                    rmsnorm_T(ti, ns[ti], col0[ti], "gmix", xnR)

                def fm_proj(W, rW, kc, wc0, src, src_res_of_seg, evac, rot=None):
                    for (s0, sl, tis) in segs:
                        pb, rpb = (rot or pmmR).next()
                        rr = [rW] + src_res_of_seg(tis)
                        for c in range(kc):
                            mm(pb[:, 0:sl], W[:, c, wc0:wc0 + 128], src[:, c, s0:s0 + sl], c == 0, c == kc - 1, rr, [rpb])
                        evac(pb, rpb, s0, sl, tis)

                def tm_proj(W, rW, kc, ncol, src, src_res, ti, evac, wc0=0):
                    n = ns[ti]
                    pb, rpb = pmmR.next()
                    for c in range(kc):
                        mm(pb[0:n, 0:ncol], src[:, c, col0[ti]:col0[ti] + n], W[:, c, wc0:wc0 + ncol], c == 0, c == kc - 1,
                           [rW, src_res], [rpb])
                    evac(pb, rpb, n)

                hres = lambda tis: [r_hT[t] for t in tis]

                W, rW = w_next()
                for ti, T in enumerate(grp):
                    def ev(pb, rpb, n, T=T):
                        P.op("act", "activation", dict(out=Vc[0:n, T, :, 0:64],
                                                           in_=pb[0:n, 0:512].rearrange("p (h d) -> p h d", h=8), func=AF.Copy),
                             reads=[rpb], writes=[r_V[gi]])
                    tm_proj(W, rW, 8, 512, actA, r_hT[ti], ti, ev)
                W, rW = w_next()
                for p in range(4):
                    def ev(pb, rpb, s0, sl, tis, p=p):
                        P.op("act", "activation", dict(out=fqT[0:64, 2 * p, s0:s0 + sl], in_=pb[0:64, 0:sl], func=AF.Copy),
                             reads=[rpb], writes=[r_fq[p]])
                        P.op("act", "activation", dict(out=fqT[64:128, 2 * p + 1, s0:s0 + sl], in_=pb[64:128, 0:sl], func=AF.Copy),
                             reads=[rpb], writes=[r_fq[p]])
                    fm_proj(W, rW, 8, p * 128, actA, hres, ev)
                W, rW = w_next()
                for p in range(4):
                    def ev(pb, rpb, s0, sl, tis, p=p):
                        P.op("act", "activation", dict(out=kTc[:, p, kv0 + s0:kv0 + s0 + sl], in_=pb[:, 0:sl], func=AF.Copy),
                             reads=[rpb], writes=[r_kT[p][gi]])
                    fm_proj(W, rW, 8, p * 128, actA, hres, ev)
                W, rW = w_next()
                r_lzt = [R("lzt%d" % i) for i in range(5)]
                for ti, T in enumerate(grp):
                    def ev(pb, rpb, n, T=T, ti=ti):
                        P.op("dve", "tensor_tensor", dict(out=lzt[0:n, ti, 0:8], in0=pb[0:n, 0:8], in1=C("bfb")[0:n, :], op=ALU.add),
                             reads=[rpb, r_const], writes=[r_lzt[ti]])
                        P.op("act", "activation", dict(out=lzt[0:n, ti, 0:8], in_=lzt[0:n, ti, 0:8], func=AF.Exp, scale=-1.0),
                             reads=[r_lzt[ti]], writes=[r_lzt[ti]])
                        P.op("act", "activation", dict(out=lzt[0:n, ti, 8:16], in_=lzt[0:n, ti, 0:8], func=AF.Ln, bias=1.0),
                             reads=[r_lzt[ti]], writes=[r_lzt[ti]])
                    tm_proj(W, rW, 8, 8, actA, r_hT[ti], ti, ev)

                def cumsum_tile(ti, T):
                    n = ns[ti]
                    first = (T == 0)
                    mm(pcum[0:n, :], C("tri")[0:n, 0:n], lzt[0:n, ti, 8:16], True, first, [r_lzt[ti], r_const], [r_pcum])
                    if not first:
                        npv = tile_n(T - 1)
                        sel = C("sel15") if npv == 16 else C("sel127")
                        mm(pcum[0:n, :], sel[0:npv, 0:n], Gc[0:npv, T - 1, :], False, True, [r_G[T - 1], r_const], [r_pcum])
                    P.op("dve", "tensor_copy", dict(out=Gc[0:n, T, :], in_=pcum[0:n, :]), reads=[r_pcum], writes=[r_G[T]])

                def rotary(pb, rpb, n, ti):
                    ps8 = pb[0:n, 0:512].rearrange("p (g d) -> p g d", g=8)
                    cb_ = cosg[0:n, ti * 64:(ti + 1) * 64].unsqueeze(1).to_broadcast([n, 8, 64])
                    sb_ = sing[0:n, ti * 64:(ti + 1) * 64].unsqueeze(1).to_broadcast([n, 8, 64])
                    r_, rr_ = r32R.next()
                    (ta, rta), (tb, rtb) = tqR.next(), tqR.next()
                    P.op("dve", "tensor_tensor", dict(out=ta[0:n, :].rearrange("p (g d) -> p g d", g=8), in0=ps8, in1=cb_, op=ALU.mult),
                         reads=[rpb, r_cs], writes=[rta])
                    P.op("dve", "tensor_tensor", dict(out=tb[0:n, :].rearrange("p (g d) -> p g d", g=8), in0=ps8, in1=sb_, op=ALU.mult),
                         reads=[rpb, r_cs], writes=[rtb])
                    A4 = ta[0:n, :].rearrange("p (h t d) -> p h t d", h=4, t=2)
                    B4 = tb[0:n, :].rearrange("p (h t d) -> p h t d", h=4, t=2)
                    r4 = r_[0:n, :].rearrange("p (h t d) -> p h t d", h=4, t=2)
                    P.op("dve", "tensor_tensor", dict(out=r4[:, :, 0, :], in0=A4[:, :, 0, :], in1=B4[:, :, 1, :], op=ALU.subtract),
                         reads=[rta, rtb], writes=[rr_])
                    P.op("dve", "tensor_tensor", dict(out=r4[:, :, 1, :], in0=B4[:, :, 0, :], in1=A4[:, :, 1, :], op=ALU.add),
                         reads=[rta, rtb], writes=[rr_])
                    return r_, rr_

                W, rW = w_next()
                for ti, T in enumerate(grp):
                    def ev(pb, rpb, n, ti=ti):
                        r_, rr_ = rotary(pb, rpb, n, ti)
                        for h in range(4):
                            P.op("act", "activation", dict(out=rq[0:n, ti, h * 128:(h + 1) * 128], in_=r_[0:n, h * 128:(h + 1) * 128],
                                                           func=AF.Copy, scale=C("qds")[0:n, h:h + 1]),
                                 reads=[rr_, r_const], writes=[r_rq[ti]])
                    tm_proj(W, rW, 8, 512, actA, r_hT[ti], ti, ev)
                W, rW = w_next()
                for ti, T in enumerate(grp):
                    def ev(pb, rpb, n, ti=ti, T=T):
                        r_, rr_ = rotary(pb, rpb, n, ti)
                        P.op("act", "activation", dict(out=rk[0:n, ti, :], in_=r_[0:n, :], func=AF.Copy),
                             reads=[rr_], writes=[r_rk[ti]])
                        kd = C("kd16") if T == 0 else C("kd128")
                        for h in range(4):
                            P.op("act", "activation", dict(out=kdk[0:n, ti, h * 128:(h + 1) * 128], in_=r_[0:n, h * 128:(h + 1) * 128],
                                                           func=AF.Copy, scale=kd[0:n, h:h + 1]),
                                 reads=[rr_, r_const], writes=[r_kdk[ti]])
                    tm_proj(W, rW, 8, 512, actA, r_hT[ti], ti, ev)
                    cumsum_tile(ti, T)
                for hf in range(2):
                    W, rW = w_next()
                    for ti, T in enumerate(grp):
                        def ev(pb, rpb, n, ti=ti, hf=hf):
                            P.op("act", "activation", dict(out=rv[0:n, ti, hf * 512:(hf + 1) * 512], in_=pb[0:n, 0:512], func=AF.Copy),
                                 reads=[rpb], writes=[r_rv[ti]])
                        tm_proj(W, rW, 8, 512, actA, r_hT[ti], ti, ev)
                for hf in range(2):
                    W, rW = w_next()
                    for ti, T in enumerate(grp):
                        def ev(pb, rpb, n, ti=ti, hf=hf):
                            P.op("act", "activation", dict(out=rgs[0:n, ti, hf * 512:(hf + 1) * 512], in_=pb[0:n, 0:512], func=AF.Silu),
                                 reads=[rpb], writes=[r_rgs[ti]])
                        tm_proj(W, rW, 8, 512, actA, r_hT[ti], ti, ev)
                def ret_stage1(ti_):
                    n_ = ns[ti_]
                    for h in range(4):
                        transp(pT[:, h, 0:n_], rq[0:n_, ti_, h * 128:(h + 1) * 128], n_, [r_rq[ti_]], [r_pT])
                    for h in range(4):
                        transp(pT[:, 4 + h, 0:n_], rk[0:n_, ti_, h * 128:(h + 1) * 128], n_, [r_rk[ti_]], [r_pT])
                    P.op("act", "activation", dict(out=qkT[:, :, 0:n_], in_=pT[:, :, 0:n_], func=AF.Copy), reads=[r_pT], writes=[r_qkT])

                ret_state = {"pend": None}

                def make_ret(ti, T):
                    n = ns[ti]
                    c0 = col0[ti]
                    Mt = C("M16") if T == 0 else C("M128")
                    Mt3 = Mt.rearrange("p (h i) -> p h i", h=4)
                    bankC, rC = pmmR.items[2]
                    por = [(pOk[0], r_pO[0]), (pOk[1], r_pO[1])]
                    Xb = [(pSk[0], r_pS[0]), (pSk[1], r_pS[1])]
                    vvs = [rv[0:n, ti, h * 256:(h + 1) * 256] for h in range(4)]
                    aMs = {}

                    def St(h):
                        mm(bankC[:, (h % 2) * 256:(h % 2) * 256 + 256], kdk[0:n, ti, h * 128:(h + 1) * 128], vvs[h], True, True,
                           [r_kdk[ti], r_rv[ti]], [rC])

                    def Supd(h):
                        src = bankC[:, (h % 2) * 256:(h % 2) * 256 + 256]
                        if T == 0:
                            P.op("dve", "tensor_copy", dict(out=S32[:, h, :], in_=src), reads=[rC], writes=[r_S32[h]])
                        else:
                            P.op("dve", "scalar_tensor_tensor", dict(out=S32[:, h, :], in0=S32[:, h, :], scalar=cd[h], in1=src,
                                                                     op0=ALU.mult, op1=ALU.add), reads=[rC, r_S32[h]], writes=[r_S32[h]])
                        P.op("act", "activation", dict(out=Sb[:, h, :], in_=S32[:, h, :], func=AF.Copy), reads=[r_S32[h]], writes=[r_Sb[h]])

                    def aT(h):
                        pa, rpa = Xb[h % 2]
                        mm(pa[0:n, 0:n], qkT[:, 4 + h, 0:n], qkT[:, h, 0:n], True, True, [r_qkT], [rpa])

                    def aMk(h):
                        pa, rpa = Xb[h % 2]
                        aM, raM = aMR.next()
                        aMs[h] = (aM, raM)
                        P.op("dve", "tensor_tensor", dict(out=aM[0:n, 0:n], in0=pa[0:n, 0:n], in1=Mt3[0:n, h, 0:n], op=ALU.mult),
                             reads=[rpa, r_const], writes=[raM])

                    def omm(h):
                        aM, raM = aMs[h]
                        pb, rpb = por[h // 2]
                        oc = (h % 2) * 256
                        mm(pb[0:n, oc:oc + 256], aM[0:n, 0:n], vvs[h], True, T == 0, [raM, r_rv[ti]], [rpb])
                        if T != 0:
                            mm(pb[0:n, oc:oc + 256], qkT[:, h, 0:n], Sb[:, h, :], False, True, [r_qkT, r_Sb[h]], [rpb])

                    def segA():
                        St(0); St(1)
                        aT(0); aT(1)
                        aMk(0); aMk(1)
                        aT(2); aT(3)
                        aMk(2); aMk(3)

                    def segB():
                        omm(0); omm(1); omm(2); omm(3)
                        Supd(0); Supd(1)
                        St(2); St(3)
                        Supd(2); Supd(3)
                        if ti + 1 < nt:
                            ret_stage1(ti + 1)

                    def segC():
                        if ret_state["pend"] is not None:
                            ret_state["pend"]()
                            ret_state["pend"] = None
                        for h in range(4):
                            pb, rpb = por[h // 2]
                            oc = (h % 2) * 256
                            P.op("dve", "bn_stats", dict(out=bnst[0:n, h, :], in_=pb[0:n, oc:oc + 256]), reads=[rpb], writes=[r_bn])
                            P.op("dve", "bn_aggr", dict(out=bnmv[0:n, h, :], in_=bnst[0:n, h, :]), reads=[r_bn], writes=[r_bn])
                        P.op("act", "activation", dict(out=hrs[0:n, 0:4], in_=bnmv[0:n, :, 1], func=AF.Sqrt, bias=EPS), reads=[r_bn], writes=[r_bn])
                        P.op("dve", "reciprocal", dict(out=hrs[0:n, 4:8], in_=hrs[0:n, 0:4]), reads=[r_bn], writes=[r_bn])
                        P.op("dve", "scalar_tensor_tensor", dict(out=hrs[0:n, 8:12], in0=bnmv[0:n, :, 0], scalar=-1.0, in1=hrs[0:n, 4:8],
                                                                 op0=ALU.mult, op1=ALU.mult), reads=[r_bn], writes=[r_bn])
                        for h in range(4):
                            pb, rpb = por[h // 2]
                            oc = (h % 2) * 256
                            P.op("act", "activation", dict(out=on16[0:n, h * 256:(h + 1) * 256], in_=pb[0:n, oc:oc + 256], func=AF.Identity,
                                                           scale=hrs[0:n, 4 + h:5 + h], bias=hrs[0:n, 8 + h:9 + h]),
                                 reads=[rpb, r_bn], writes=[r_on32])
                        rt_, rrt = rtokR.next()
                        P.op("pool", "tensor_tensor", dict(out=rt_[0:n, :], in0=on16[0:n, :], in1=rgs[0:n, ti, :], op=ALU.mult),
                             reads=[r_on32, r_rgs[ti]], writes=[rrt])

                        def fin_ret():
                            for c in range(8):
                                transp(pT[:, c, 0:n], rt_[0:n, c * 128:(c + 1) * 128], n, [rrt], [r_pT])
                            P.op("act", "activation", dict(out=retT[:, :, c0:c0 + n], in_=pT[:, :, 0:n], func=AF.Copy), reads=[r_pT], writes=[r_ret[ti]])
                        ret_state["pend"] = fin_ret
                    return [segA, segB, segC]

                ret_segs = [lambda: ret_stage1(0)]
                for ti, T in enumerate(grp):
                    ret_segs += make_ret(ti, T)
                gate_rot = Rot([pmmR.items[0], pmmR.items[1]])

                for which, (gT_, rg_) in enumerate([(gaT, r_ga), (grT, r_gr)]):
                    for hf in range(2):
                        W, rW = w_next()
                        for b4 in range(4):
                            blk = hf * 4 + b4
                            def ev(pb, rpb, s0, sl, tis, blk=blk, gT_=gT_, rg_=rg_, which=which):
                                bcol = C("bbr")[:, which * 8 + blk:which * 8 + blk + 1]
                                P.op("act", "activation", dict(out=gT_[:, blk, s0:s0 + sl], in_=pb[:, 0:sl], func=AF.Sigmoid, bias=bcol),
                                     reads=[rpb, r_const], writes=[rg_[blk]])
                            fm_proj(W, rW, 8, b4 * 128, actA, hres, ev, rot=gate_rot)
                            if ret_segs:
                                ret_segs.pop(0)()
                while ret_segs:
                    ret_segs.pop(0)()

                if gcount == 1:
                    dump("hT", actA[:, :, 0:G], [128, 8, G], BF16, r_hT[:nt])
                    dump("fqT", fqT[:, :, 0:G], [128, 8, G], BF16, r_fq)
                    dump("kT", kTc[:, :, 0:G], [128, 4, G], BF16, [r_kT[p][0] for p in range(4)])
                    dump("V", Vc[:, 0:5, :, :], [128, 5, 8, 65], BF16, [r_V[0]])
                    dump("G", Gc[:, 0:5, :], [128, 5, 8], F32, r_G[:5])
                    dump("rq", rq[:, :, :], [128, 5, 512], BF16, r_rq)
                    dump("rk", rk[:, :, :], [128, 5, 512], BF16, r_rk)
                    dump("kdk", kdk[:, :, :], [128, 5, 512], BF16, r_kdk)
                    dump("rv", rv[:, :, :], [128, 5, 1024], BF16, r_rv)
                    dump("rgs", rgs[:, :, :], [128, 5, 1024], BF16, r_rgs)
                    dump("gaT", gaT[:, :, 0:G], [128, 8, G], BF16, r_ga)
                    dump("grT", grT[:, :, 0:G], [128, 8, G], BF16, r_gr)
                qsets = [[0], [1, 2, 3, 4]] if gi == 0 else [[0, 1, 2, 3]]
                for Q in qsets:
                    NQ = sum(ns[t_] for t_ in Q)
                    qc0 = col0[Q[0]]
                    Tq = [grp[t_] for t_ in Q]
                    qcs = [sum(ns[t_] for t_ in Q[:i_]) for i_ in range(len(Q))]
                    for qi, ti in enumerate(Q):
                        T = grp[ti]
                        n = ns[ti]
                        selv = C("selm")[0:n, 0:4] if T == 0 else C("selq")[0:n, qi * 4:qi * 4 + 4]
                        mm(pgref[0:4, :], selv, Gc[0:n, T, :], qi == 0, qi == len(Q) - 1, [r_G[T], r_const], [r_pgref])
                    gv, gr1, gr2 = gpc[0:4, 0:8], gpc[0:4, 8:16], gpc[0:4, 16:24]
                    ghi, gmid, glo = gpb[0:4, 0:8], gpb[0:4, 8:16], gpb[0:4, 16:24]
                    P.op("dve", "tensor_scalar", dict(out=gv, in0=pgref[0:4, :], scalar1=-8.0, scalar2=None, op0=ALU.mult), reads=[r_pgref], writes=[r_gref])
                    P.op("dve", "tensor_copy", dict(out=ghi, in_=gv), reads=[r_gref], writes=[r_gref])
                    P.op("dve", "tensor_tensor", dict(out=gr1, in0=gv, in1=ghi, op=ALU.subtract), reads=[r_gref], writes=[r_gref])
                    P.op("dve", "tensor_copy", dict(out=gmid, in_=gr1), reads=[r_gref], writes=[r_gref])
                    P.op("dve", "tensor_tensor", dict(out=gr2, in0=gr1, in1=gmid, op=ALU.subtract), reads=[r_gref], writes=[r_gref])
                    P.op("dve", "tensor_copy", dict(out=glo, in_=gr2), reads=[r_gref], writes=[r_gref])
                    for pi, gp_ in enumerate([ghi, gmid, glo]):
                        P.op("dve", "tensor_copy", dict(out=cpT[32 * pi:32 * pi + 4, :, :], in_=gp_.unsqueeze(2).to_broadcast([4, 8, 128])),
                             reads=[r_gref], writes=[r_bias])
                    Tmax = Tq[-1]
                    blocks = [(h, kt) for h in range(8) for kt in range(Tmax + 1)]
                    sbanks = [(pSk[0], r_pS[0]), (pSk[1], r_pS[1]), pmmR.items[0], pmmR.items[1]]
                    NB = len(blocks)
                    pts, pend = {}, {}

                    def split(kt):
                        i0 = min(i_ for i_ in range(len(Q)) if Tq[i_] >= kt)
                        cs = qcs[i0]
                        diag = (Tq[i0] == kt)
                        nd = ns[Q[i0]]
                        return i0, cs, diag, nd

                    def S_stage(bi):
                        h, kt = blocks[bi]
                        p, base = h // 2, 64 * (h % 2)
                        nk = tile_n(kt)
                        kc0 = tile_pos(kt)
                        kg = 0 if kt <= 4 else (kt - 1) // 4
                        i0, cs, diag, nd = split(kt)
                        bank, rbank = sbanks[bi % 4]
                        kTv = kTc[:, p, kc0:kc0 + nk]
                        rr = [r_kT[p][kg], r_fq[p]]
                        cpv = cpT[:, h, 0:nk]
                        if diag:
                            mm(bank[0:nk, cs:cs + nd], kTv, fqT[:, h, qc0 + cs:qc0 + cs + nd], True, False, rr, [rbank])
                            mm(bank[0:nk, cs:cs + nd], identb[0:nk, 0:nk], maskT[0:nk, 0:nd], False, False, [r_const], [rbank])
                            mm(bank[0:nk, cs:cs + nd], cpv, indq[:, cs:cs + nd], False, True, [r_const, r_bias], [rbank])
                            if cs + nd < NQ:
                                mm(bank[0:nk, cs + nd:NQ], kTv, fqT[:, h, qc0 + cs + nd:qc0 + NQ], True, False, rr, [rbank])
                                mm(bank[0:nk, cs + nd:NQ], cpv, indq[:, cs + nd:NQ], False, True, [r_const, r_bias], [rbank])
                        else:
                            mm(bank[0:nk, cs:NQ], kTv, fqT[:, h, qc0 + cs:qc0 + NQ], True, False, rr, [rbank])
                            mm(bank[0:nk, cs:NQ], cpv, indq[:, cs:NQ], False, True, [r_const, r_bias], [rbank])
                        pt, rpt = ptR.next()
                        P.op("act", "activation", dict(
                            out=pt[0:nk, cs:NQ], in_=bank[0:nk, cs:NQ], func=AF.Exp, bias=Gc[0:nk, kt, h:h + 1], scale=0.125),
                             reads=[rbank, r_G[kt]], writes=[rpt])
                        pts[bi] = (pt, rpt)

                    def PV_stage(bi, step):
                        h, kt = blocks[bi]
                        p, base = h // 2, 64 * (h % 2)
                        nk = tile_n(kt)
                        kg = 0 if kt <= 4 else (kt - 1) // 4
                        i0, cs, diag, nd = split(kt)
                        po, rpo = pOk[h % 2], r_pO[h % 2]
                        pt, rpt = pts.pop(bi)
                        nkp = 128 if kt == 0 else nk
                        Vv = Vc[0:nkp, kt, h, 0:65]
                        if diag:
                            mm(po[0:65, cs:cs + nd], Vv, pt[0:nkp, cs:cs + nd], kt == 0, True, [r_V[kg], rpt], [rpo])
                            if cs + nd < NQ:
                                mm(po[0:65, cs + nd:NQ], Vv, pt[0:nkp, cs + nd:NQ], kt == 0, False, [r_V[kg], rpt], [rpo])
                        else:
                            mm(po[0:65, cs:NQ], Vv, pt[0:nkp, cs:NQ], kt == 0, False, [r_V[kg], rpt], [rpo])
                        if kt == Tmax:
                            o_, ro_ = o32R.next()
                            P.op("act", "activation", dict(out=o_[0:65, 0:NQ], in_=po[0:65, 0:NQ], func=AF.Copy),
                                 reads=[rpo], writes=[ro_])
                            P.op("dve", "reciprocal", dict(out=o_[64:65, 0:NQ], in_=o_[64:65, 0:NQ]), reads=[ro_], writes=[ro_])

                            def norm2(o_=o_, ro_=ro_, p=p, base=base):
                                pr, rpr = pmmR.items[2]
                                mm(pr[0:64, 0:NQ], C("ones")[:, 0:64], o_[:, 0:NQ], True, True, [ro_, r_const], [rpr])
                                P.op("dve", "tensor_tensor", dict(
                                    out=foxT[base:base + 64, p, qc0:qc0 + NQ], in0=o_[0:64, 0:NQ], in1=pr[0:64, 0:NQ], op=ALU.mult),
                                     reads=[ro_, rpr], writes=[r_fox[t_] for t_ in Q])
                            pend.setdefault(step + min(12, 2 * (Tmax + 1)), []).append(norm2)

                    LA = 3
                    for step in range(NB + LA + 14):
                        for f_ in pend.pop(step, []):
                            f_()
                        if step < NB:
                            S_stage(step)
                        if 0 <= step - LA < NB:
                            PV_stage(step - LA, step)
                    assert not pend and not pts

                if gcount == 1:
                    dump("foxT", foxT[:, :, 0:G], [128, 4, G], BF16, r_fox)
                    dump("retT", retT[:, :, 0:G], [128, 8, G], BF16, r_ret)
                P.op("pool", "memset", dict(ap=stat[:, 61:62], constant=0.0),
                     writes=r_m12 + r_upe + r_uph + r_y0 + r_y2b + r_outb + r_xnD + r_rq + r_rk + r_kdk + r_rv + r_rgs)

                Wfo, rWfo = w_next()
                m12R = Rot([(m12[i], r_m12[i]) for i in range(4)])
                fres = lambda tis: [r_fox[t] for t in tis]
                rres = lambda tis: [r_ret[t] for t in tis]
                for cb_ in range(8):
                    for (s0, sl, tis) in segs:
                        pa1, rp1 = pmmR.next()
                        for c in range(4):
                            mm(pa1[:, 0:sl], Wfo[:, c, cb_ * 128:(cb_ + 1) * 128], foxT[:, c, s0:s0 + sl], c == 0, c == 3, [rWfo] + fres(tis), [rp1])
                        P.op("dve", "tensor_tensor", dict(out=actA[:, cb_, s0:s0 + sl], in0=pa1[:, 0:sl], in1=gaT[:, cb_, s0:s0 + sl], op=ALU.mult),
                             reads=[rp1, r_ga[cb_]], writes=[r_hT[t] for t in tis])
                if ret_state["pend"] is not None:
                    ret_state["pend"]()
                    ret_state["pend"] = None
                for hf in range(2):
                    Wro, rWro = w_next()
                    for cb_ in range(hf * 4, hf * 4 + 4):
                        for (s0, sl, tis) in segs:
                            pa2, rp2 = pmmR.next()
                            for c in range(8):
                                mm(pa2[:, 0:sl], Wro[:, c, (cb_ % 4) * 128:(cb_ % 4 + 1) * 128], retT[:, c, s0:s0 + sl], c == 0, c == 7, [rWro] + rres(tis), [rp2])
                            m2, rm2 = m12R.next()
                            P.op("dve", "tensor_tensor", dict(out=m2[:, 0:sl], in0=pa2[:, 0:sl], in1=grT[:, cb_, s0:s0 + sl], op=ALU.mult),
                                 reads=[rp2, r_gr[cb_]], writes=[rm2])
                            P.op("dve", "tensor_tensor", dict(out=actA[:, cb_, s0:s0 + sl], in0=m2[:, 0:sl], in1=actA[:, cb_, s0:s0 + sl], op=ALU.add),
                                 reads=[rm2] + [r_hT[t] for t in tis], writes=[r_hT[t] for t in tis])
                for hf in range(2):
                    W, rW = w_next()
                    for ti, T in enumerate(grp):
                        def ev(pb, rpb, n, ti=ti, hf=hf):
                            P.op("dve", "tensor_tensor", dict(out=xres[0:n, ti, hf * 512:(hf + 1) * 512], in0=pb[0:n, 0:512],
                                                                  in1=xres[0:n, ti, hf * 512:(hf + 1) * 512], op=ALU.add),
                                 reads=[rpb, r_x[ti]], writes=[r_x[ti]])
                        tm_proj(W, rW, 8, 512, actA, r_hT[ti], ti, ev)

                if gcount == 1:
                    dump("x1", xres[:, :, :], [128, 5, D], F32, r_x)
                P.op("pool", "memset", dict(ap=stat[:, 62:63], constant=0.0), writes=r_act + r_ga + r_gr + r_ret)

                for ti, T in enumerate(grp):
                    rmsnorm_T(ti, ns[ti], col0[ti], "gffn", xnDR)
                cw3 = C("cw").rearrange("p (b k) -> p b k", k=3)
                cbv = C("cbias")
                upR = Rot([(upe[i], r_upe[i], r_uph[i]) for i in range(3)])
                y0R = Rot([(y0b[i], r_y0[i]) for i in range(4)])
                y2R = Rot([(y2b[i], r_y2b[i]) for i in range(3)])
                W, rW = None, None
                fin = {}
                SKEW = 2
                order = []
                for j_ in range(11):
                    order += [(0, 2 * j_), (1, 2 * j_ + 1), (2, 22 + 2 * j_), (3, 22 + 2 * j_ + 1)]
                for k_ in range(44 + SKEW):
                    if k_ < 44:
                        wi, b = order[k_]
                        if wi == 0:
                            W, rW = w_next()
                        f = b % 22
                        up_, rup, ruh = upR.next()
                        y0_, ry0 = y0R.next()
                        P.op("dve", "tensor_copy", dict(out=up_[:, 0:2], in_=halo[:, b, :]), reads=[r_halo[b]], writes=[ruh])
                        for (s0, sl, tis) in segs:
                            pb, rpb = pmmR.next()
                            for c in range(8):
                                mm(pb[:, 0:sl], W[:, c, wi * 128:(wi + 1) * 128], actA[:, c, s0:s0 + sl], c == 0, c == 7, [rW] + hres(tis), [rpb])
                            P.op("act", "activation", dict(out=up_[:, 2 + s0:2 + s0 + sl], in_=pb[:, 0:sl], func=AF.Copy),
                                 reads=[rpb], writes=[rup])
                            P.op("act", "activation", dict(out=y0_[:, s0:s0 + sl], in_=pb[:, 0:sl], func=AF.Identity,
                                                           scale=cw3[:, b, 2:3], bias=cbv[:, b:b + 1]),
                                 reads=[rpb, r_const], writes=[ry0])
                        P.op("dve", "scalar_tensor_tensor", dict(out=y0_[:, 0:G], in0=up_[:, 1:1 + G], scalar=cw3[:, b, 1:2], in1=y0_[:, 0:G],
                                                                 op0=ALU.mult, op1=ALU.add), reads=[rup, ruh, ry0, r_const], writes=[ry0])
                        if b < 22:
                            yo_, ryo = y0_, ry0
                        else:
                            yo_, ryo = y2R.next()
                        P.op("dve", "scalar_tensor_tensor", dict(out=yo_[:, 0:G], in0=up_[:, 0:G], scalar=cw3[:, b, 0:1], in1=y0_[:, 0:G],
                                                                 op0=ALU.mult, op1=ALU.add), reads=[rup, ruh, ry0, r_const], writes=[ryo] if b >= 22 else [ry0])
                        P.op("dve", "tensor_copy", dict(out=halo[:, b, :], in_=up_[:, G:G + 2]), reads=[rup], writes=[r_halo[b]])
                        fin[k_] = (yo_, ryo, f, b)
                    bb = k_ - SKEW
                    if bb >= 0:
                        y0_, ry0, f, b_fin = fin.pop(bb)
                        if b_fin < 22:
                            P.op("act", "activation", dict(out=actT[:, f, 0:G], in_=y0_[:, 0:G], func=AF.Silu), reads=[ry0], writes=[r_act[f]])
                        else:
                            P.op("dve", "tensor_tensor", dict(out=actT[:, f, 0:G], in0=y0_[:, 0:G], in1=actT[:, f, 0:G], op=ALU.mult),
                                 reads=[ry0, r_act[f]], writes=[r_act[f]])
                P.op("pool", "memset", dict(ap=stat[:, 63:64], constant=0.0), writes=r_m12 + r_upe + r_uph + r_y0 + r_outb)
                gidx = all_groups.index((seq, gi))
                nxt = all_groups[gidx + 1] if gidx + 1 < len(all_groups) else None
                if dbg is not None and gcount >= dbg.get("ngroups", 99):
                    nxt = None

                def final_tile(ti, T):
                    n = ns[ti]
                    if T != 0:
                        st_, rst = statR.next()
                        P.op("act", "activation", dict(out=junk[0:n, :], in_=xres[0:n, ti, :], func=AF.Square, accum_out=st_[0:n, 0:1]),
                             reads=[r_x[ti]], writes=[rst, r_on32])
                        P.op("act", "activation", dict(out=st_[0:n, 1:2], in_=st_[0:n, 0:1], func=AF.Sqrt, scale=1.0 / D, bias=EPS),
                             reads=[rst], writes=[rst])
                        P.op("dve", "reciprocal", dict(out=st_[0:n, 2:3], in_=st_[0:n, 1:2]), reads=[rst], writes=[rst])
                        ob, rob = outb[ti], r_outb[ti]
                        P.op("dve", "scalar_tensor_tensor", dict(out=ob[0:n, :], in0=xres[0:n, ti, :], scalar=st_[0:n, 2:3], in1=gfb[0:n, :],
                                                                 op0=ALU.mult, op1=ALU.mult), reads=[r_x[ti], rst, r_const], writes=[rob])
                        P.dma("sp", out_d[seq, 128 * (T - 1):128 * T, :], ob[0:n, :], reads=[rob], writes=[R("outdram")], key=("o", ti))
                    if nxt is not None:
                        ngrp = GROUPS[nxt[1]]
                        if ti < len(ngrp):
                            xload(nxt[0], ngrp[ti], ti)
                            prefetched.add(ti)

                for q8 in range(8):
                    W, rW = w_next()
                    for ti, T in enumerate(grp):
                        n = ns[ti]
                        pb, rpb = pmmR.next()
                        for f in range(22):
                            mm(pb[0:n, 0:128], actT[:, f, col0[ti]:col0[ti] + n], W[:, f, 0:128], f == 0, f == 21, [rW, r_act[f]], [rpb])
                        P.op("dve", "tensor_tensor", dict(out=xres[0:n, ti, q8 * 128:(q8 + 1) * 128], in0=pb[0:n, 0:128],
                                                          in1=xres[0:n, ti, q8 * 128:(q8 + 1) * 128], op=ALU.add),
                             reads=[rpb, r_x[ti]], writes=[r_x[ti]])
                        if q8 == 7:
                            if gcount == 1 and ti == nt - 1:
                                dump("x2", xres[:, :, :], [128, 5, D], F32, r_x)
                            final_tile(ti, T)
        P.emit()
    nc._n_ops = len(P.ops)
    return nc


def _prep_inputs(x, meta, norm_mix_g, w_in, b_forget, b_branch, w_fox_o, w_ret_o, w_out,
                 norm_ffn_g, w_up, conv_w, conv_b, w_down, norm_f_g):
    f = lambda a: np.ascontiguousarray(np.asarray(a, dtype=np.float32))
    cfh, cbh = make_consts()
    cfa = np.zeros((128, CF_TOT), np.float32)
    for k in ["tri", "sel0", "selq", "selm", "sel15", "sel127", "ones", "qds", "kd128", "kd16", "M128", "M16"]:
        cfa[:, CF_OFF[k]:CF_OFF[k] + CF_W[k]] = cfh[k]
    cfa[:, CF_OFF["gmix"]:CF_OFF["gmix"] + 8] = f(norm_mix_g)[0].reshape(8, 128).T
    cfa[:, CF_OFF["gffn"]:CF_OFF["gffn"] + 8] = f(norm_ffn_g)[0].reshape(8, 128).T
    cfa[:, CF_OFF["bfb"]:CF_OFF["bfb"] + 8] = np.broadcast_to(f(b_forget)[0][None, :], (128, 8))
    cfa[:, CF_OFF["bbr"]:CF_OFF["bbr"] + 16] = f(b_branch)[0].reshape(16, 128).T
    cw = f(conv_w)[0]
    cfa[:, CF_OFF["cw"]:CF_OFF["cw"] + 132] = cw.reshape(3, 44, 128).transpose(2, 1, 0).reshape(128, 132)
    cfa[:, CF_OFF["cbias"]:CF_OFF["cbias"] + 44] = f(conv_b)[0].reshape(44, 128).T
    cba = np.concatenate([cbh["ident"], cbh["maskT"], cbh["ind"]], axis=1)
    common = dict(meta=f(meta), w_in=f(w_in)[0], w_fox_o=f(w_fox_o)[0], w_ret_o=f(w_ret_o)[0], w_out=f(w_out)[0],
                  w_up=f(w_up)[0], w_down=f(w_down)[0], cf=cfa, cb=np.ascontiguousarray(cba),
                  gfb=np.ascontiguousarray(np.broadcast_to(f(norm_f_g)[None, :], (128, D))),
                  cos=cfh["cos"], sin=cfh["sin"])
    xs = f(x)
    in_maps = []
    for c in range(NCORES):
        m = dict(common)
        m["x"] = np.ascontiguousarray(xs[c * SPC:(c + 1) * SPC])
        in_maps.append(m)
    return in_maps


def kernel(**inputs):
    in_maps = _prep_inputs(**inputs)
    nc = build_nc()
    res = run_bass_kernel_spmd(nc, in_maps, core_ids=list(range(NCORES)))
    out = np.concatenate([np.asarray(r["out"]) for r in res.results], axis=0)
    return out.astype(np.float32)
```

```python
import contextlib
import numpy as np
import ml_dtypes
import concourse.bass as bass
import concourse.mybir as mybir
from concourse.bass_utils import run_bass_kernel_spmd

F32 = mybir.dt.float32
BF16 = mybir.dt.bfloat16
AF = mybir.ActivationFunctionType
ALU = mybir.AluOpType

D = 1024
BATCH = 16
SEQ = 2048
NMETA = 16
NCORES = 8
SPC = BATCH // NCORES
DIN = 6664
DFF = 2816
EPS = 1e-6
NT = 17
GROUPS = [[0, 1, 2, 3, 4], [5, 6, 7, 8], [9, 10, 11, 12], [13, 14, 15, 16]]
GMAX = 528
NEGM = -30000.0
WSLOT = 4096
NSLOT = 3


def tile_n(T):
    return NMETA if T == 0 else 128


def tile_pos(T):
    return 0 if T == 0 else NMETA + 128 * (T - 1)


class Res:
    __slots__ = ("name", "last_w", "readers", "excl")

    def __init__(self, name, excl=False):
        self.name = name
        self.last_w = None
        self.readers = []
        self.excl = excl


class Op:
    __slots__ = ("eng", "fn", "deps", "sig", "key", "val", "is_dma", "idx")


class Prog:
    ENG = ("pe", "act", "dve", "pool", "sp")

    def __init__(self, nc):
        self.nc = nc
        self.ops = []
        self.same_engine_sync = {"pe": False, "act": True, "dve": True, "pool": True, "sp": False}

    def res(self, name):
        return Res(name)

    def op(self, eng, meth, kw, reads=(), writes=(), dma_key=None):
        o = Op()
        o.eng = eng
        o.fn = (meth, kw)
        o.sig = False
        o.is_dma = dma_key is not None
        o.key = dma_key if o.is_dma else eng
        o.val = None
        o.idx = len(self.ops)
        deps = set()
        for r in reads:
            if r.last_w is not None:
                deps.add(r.last_w)
            if r.excl:
                for rd in r.readers:
                    if self.ops[rd].eng != eng:
                        deps.add(rd)
        for r in writes:
            if r.last_w is not None:
                deps.add(r.last_w)
            deps.update(r.readers)
        deps.discard(o.idx)
        o.deps = deps
        for r in reads:
            r.readers.append(o.idx)
        for r in writes:
            r.last_w = o.idx
            r.readers = []
        self.ops.append(o)
        return o

    def dma(self, eng, out_ap, in_ap, reads=(), writes=(), key=None):
        return self.op(eng, "dma_start", dict(out=out_ap, in_=in_ap), reads, writes, dma_key=key)

    def emit(self, final_wait_eng="sp"):
        nc = self.nc
        ops = self.ops
        for o in ops:
            need = set()
            for d in o.deps:
                p = ops[d]
                if (not p.is_dma) and p.eng == o.eng and not self.same_engine_sync[o.eng]:
                    continue
                need.add(d)
            o.deps = need
            for d in need:
                ops[d].sig = True
        for o in ops:
            if o.is_dma:
                o.sig = True
        last_of_eng = {}
        for o in ops:
            if not o.is_dma:
                last_of_eng[o.eng] = o
        for o in last_of_eng.values():
            o.sig = True
        counts = {}
        for o in ops:
            if o.sig:
                inc = 16 if o.is_dma else 1
                counts[o.key] = counts.get(o.key, 0) + inc
                o.val = counts[o.key]
        keys = sorted(counts.keys(), key=str)
        with contextlib.ExitStack() as st:
            sems = {k: st.enter_context(nc.semaphore("s_%s" % str(k).replace(" ", ""))) for k in keys}
            block = st.enter_context(nc.Block())
            per_eng = {e: [o for o in ops if o.eng == e] for e in self.ENG}

            def run_engine(ename, e):
                seen = {}
                for o in per_eng[ename]:
                    waits = {}
                    for d in o.deps:
                        p = ops[d]
                        if waits.get(p.key, 0) < p.val:
                            waits[p.key] = p.val
                    for k, v in waits.items():
                        if seen.get(k, 0) >= v:
                            continue
                        e.wait_ge(sems[k], v)
                        seen[k] = v
                    ins = getattr(e, o.fn[0])(**o.fn[1])
                    if o.sig:
                        ins.then_inc(sems[o.key], 16 if o.is_dma else 1)
                if ename == final_wait_eng:
                    for k in keys:
                        if k != ename:
                            e.wait_ge(sems[k], counts[k])

            @block.tensor
            def _(e):
                run_engine("pe", e)

            @block.scalar
            def _(e):
                run_engine("act", e)

            @block.vector
            def _(e):
                run_engine("dve", e)

            @block.gpsimd
            def _(e):
                run_engine("pool", e)

            @block.sync
            def _(e):
                run_engine("sp", e)


class Rot:
    def __init__(self, items):
        self.items = items
        self.i = 0

    def next(self):
        it = self.items[self.i % len(self.items)]
        self.i += 1
        return it


def make_consts():
    cf = {}
    tri = np.triu(np.ones((128, 128), np.float32))
    cf["tri"] = tri
    sel0 = np.zeros((128, 128), np.float32); sel0[0, :] = 1
    sel15 = np.zeros((128, 128), np.float32); sel15[15, :] = 1
    sel127 = np.zeros((128, 128), np.float32); sel127[127, :] = 1
    cf["sel0"], cf["sel15"], cf["sel127"] = sel0, sel15, sel127
    selq = np.zeros((128, 4, 4), np.float32)
    for qi in range(4):
        selq[64, qi, qi] = 1
    cf["selq"] = selq.reshape(128, 16)
    selm = np.zeros((128, 4), np.float32)
    selm[8, 0] = 1
    cf["selm"] = selm
    ones = np.zeros((128, 64), np.float32); ones[64, :] = 1
    cf["ones"] = ones
    hh = np.arange(4, dtype=np.float64)
    gam = 1.0 - 2.0 ** (-5.0 - hh)
    scale = 128.0 ** -0.5
    idx = np.arange(128, dtype=np.float64)
    qds = scale * gam[None, :] ** (idx[:, None] + 1.0)
    cf["qds"] = qds.astype(np.float32)
    cf["kd128"] = (gam[None, :] ** (127.0 - idx[:, None])).astype(np.float32)
    kd16 = np.ones((128, 4)); kd16[:16] = gam[None, :] ** (15.0 - idx[:16, None])
    cf["kd16"] = kd16.astype(np.float32)
    j = idx[:, None]; i = idx[None, :]
    M = np.zeros((128, 4, 128))
    allowed = (np.floor(j / 64) <= np.floor(i / 64))
    for h in range(4):
        M[:, h, :] = np.where(allowed, gam[h] ** (np.abs(i - j) - (i + 1.0)), 0.0)
    cf["M128"] = M.reshape(128, 512).astype(np.float32)
    M16 = np.zeros((128, 4, 16))
    for h in range(4):
        M16[:16, h, :] = gam[h] ** (np.abs(i[:, :16] - j[:16]) - (i[:, :16] + 1.0))
    cf["M16"] = M16.reshape(128, 64).astype(np.float32)
    cf["cd"] = [float(g ** 128.0) for g in gam]
    inv = 10000.0 ** (-np.arange(0, 128, 2, dtype=np.float64) / 128.0)
    cos = np.zeros((128, NT, 64)); sin = np.zeros((128, NT, 64))
    for T in range(NT):
        n = tile_n(T)
        pos = (tile_pos(T) + np.arange(n)).astype(np.float32).astype(np.float64)
        ang = (pos[:, None].astype(np.float32) * inv[None, :].astype(np.float32)).astype(np.float32)
        cos[:n, T] = np.cos(ang.astype(np.float64)); sin[:n, T] = np.sin(ang.astype(np.float64))
    cf["cos"] = cos.reshape(128, NT * 64).astype(np.float32)
    cf["sin"] = sin.reshape(128, NT * 64).astype(np.float32)
    cb = {}
    cb["ident"] = np.eye(128, dtype=np.float32).astype(ml_dtypes.bfloat16)
    maskT = np.where(np.arange(128)[:, None] <= np.arange(128)[None, :], 0.0, NEGM).astype(np.float32)
    cb["maskT"] = maskT.astype(ml_dtypes.bfloat16)
    ind = np.zeros((128, 512), np.float32)
    for r in range(4):
        for base in (0, 32, 64):
            ind[base + r, r * 128:(r + 1) * 128] = 1
    cb["ind"] = ind.astype(ml_dtypes.bfloat16)
    return cf, cb


CF_KEYS = ["tri", "sel0", "selq", "selm", "sel15", "sel127", "ones", "qds", "kd128", "kd16", "M128", "M16",
           "gmix", "gffn", "bfb", "bbr", "cw", "cbias"]
CF_W = {"tri": 128, "sel0": 128, "selq": 16, "selm": 4, "sel15": 128, "sel127": 128, "ones": 64, "qds": 4, "kd128": 4, "kd16": 4,
        "M128": 512, "M16": 64, "gmix": 8, "gffn": 8, "bfb": 8, "bbr": 16, "cw": 132, "cbias": 44}
CF_OFF = {}
_o = 0
for _k in CF_KEYS:
    CF_OFF[_k] = _o
    _o += CF_W[_k]
CF_TOT = _o


def build_nc(dbg=None):
    nc = bass.Bass("TRN2", target_bir_lowering=False)
    cfh, _ = make_consts()
    cd = cfh["cd"]

    def din(name, shape, dt=F32):
        return nc.dram_tensor(name, shape, dt, kind="ExternalInput").ap()

    x_d = din("x", [SPC, SEQ, D])
    meta_d = din("meta", [NMETA, D])
    w_in_d = din("w_in", [D, DIN])
    w_fo_d = din("w_fox_o", [512, D])
    w_ro_d = din("w_ret_o", [D, D])
    w_out_d = din("w_out", [D, D])
    w_up_d = din("w_up", [D, 2 * DFF])
    w_dn_d = din("w_down", [DFF, D])
    cf_d = din("cf", [128, CF_TOT])
    cb_d = din("cb", [128, 768], BF16)
    gf_d = din("gfb", [128, D])
    cos_d = din("cos", [128, NT * 64])
    sin_d = din("sin", [128, NT * 64])
    out_d = nc.dram_tensor("out", [SPC, SEQ, D], F32, kind="ExternalOutput").ap()

    P = Prog(nc)
    with contextlib.ExitStack() as st:
        def sb(name, shape, dt):
            return st.enter_context(nc.sbuf_tensor(name, shape, dt))

        def ps(name, shape, dt):
            return st.enter_context(nc.psum_tensor(name, shape, dt))

        cf = sb("cf_s", [128, CF_TOT], F32)
        cbt = sb("cbt", [128, 768], BF16)
        gfb = sb("gfb_s", [128, D], F32)
        cosg = sb("cosg", [128, 5 * 64], F32)
        sing = sb("sing", [128, 5 * 64], F32)
        xres = sb("xres", [128, 5, D], F32)
        actA = sb("actA", [128, 8, GMAX], BF16)
        fqT = sb("fqT", [128, 8, GMAX], BF16)
        kTc = sb("kTc", [128, 4, SEQ + NMETA], BF16)
        Vc = sb("Vc", [128, NT, 8, 65], BF16)
        Gc = sb("Gc", [128, NT, 8], F32)
        cpT = sb("cpT", [128, 8, 128], BF16)
        gpc = sb("gpc", [128, 64], F32)
        gpb = sb("gpb", [128, 32], BF16)
        big = sb("big", [128, 3 * 8 * GMAX], BF16)
        ar1 = sb("ar1", [128, 8960], F32)
        foxT = sb("foxT", [128, 4, GMAX], BF16)
        S32 = sb("S32", [128, 4, 256], F32)
        Sb = sb("Sb", [128, 4, 256], BF16)
        wsl = [sb("wsl%d" % i, [128, WSLOT], BF16) for i in range(NSLOT)]
        halo = sb("halo", [128, 44, 2], F32)
        stat = sb("stat", [128, 64], F32)
        xn2 = [sb("xn%d" % i, [128, D], BF16) for i in range(2)]
        r32 = [sb("r32_%d" % i, [128, 512], F32) for i in range(2)]
        tq = [sb("tq%d" % i, [128, 512], F32) for i in range(2)]
        ptb = [sb("ptb%d" % i, [128, 512], BF16) for i in range(4)]
        o32 = [sb("o32_%d" % i, [128, 512], F32) for i in range(2)]
        qkT = sb("qkT", [128, 8, 128], BF16)
        aMb = [sb("aM%d" % i, [128, 128], BF16) for i in range(4)]
        on16 = sb("on16", [128, D], BF16)
        junk = on16
        rtok = xn2
        lz = sb("lz", [128, 16], F32)
        lzt = sb("lzt", [128, 5, 16], F32)
        gref = sb("gref", [128, 8], F32)
        bnst = sb("bnst", [128, 4, 6], F32)
        bnmv = sb("bnmv", [128, 4, 2], F32)
        hrs = sb("hrs", [128, 16], F32)

        gaT = big[:, 0:8 * GMAX].rearrange("p (c n) -> p c n", c=8)
        grT = big[:, 8 * GMAX:16 * GMAX].rearrange("p (c n) -> p c n", c=8)
        retT = big[:, 16 * GMAX:24 * GMAX].rearrange("p (c n) -> p c n", c=8)
        actT = big[:, 0:22 * GMAX].rearrange("p (c n) -> p c n", c=22)
        ar1b = ar1.bitcast(BF16)
        rq = ar1b[:, 0:2560].rearrange("p (t n) -> p t n", t=5)
        rk = ar1b[:, 2560:5120].rearrange("p (t n) -> p t n", t=5)
        kdk = ar1b[:, 5120:7680].rearrange("p (t n) -> p t n", t=5)
        rv = ar1b[:, 7680:12800].rearrange("p (t n) -> p t n", t=5)
        rgs = ar1b[:, 12800:17920].rearrange("p (t n) -> p t n", t=5)
        m12 = [ar1[:, i * 512:(i + 1) * 512] for i in range(4)]
        upe = [ar1[:, 2048 + i * 544: 2048 + (i + 1) * 544] for i in range(3)]
        y0b = [ar1[:, 3680 + i * 544: 3680 + (i + 1) * 544] for i in range(4)]
        outb = [ar1[:, i * 1024:(i + 1) * 1024] for i in range(5)]
        xnD = [ar1b[:, 15808 + i * 1024: 15808 + (i + 1) * 1024] for i in range(2)]
        y2b = [ar1b[:, 11712 + i * 1088: 11712 + (i + 1) * 1088] for i in range(3)]

        pmm = [ps("pmm%d" % i, [128, 512], F32) for i in range(3)]
        pT = ps("pT", [128, 8, 128], BF16)
        pSk = [ps("pS%d" % i, [128, 512], F32) for i in range(2)]
        pOk = [ps("pO%d" % i, [128, 512], F32) for i in range(2)]

        R = P.res
        r_const = R("const")
        r_cs = R("cossin")
        r_x = [R("x%d" % i) for i in range(5)]
        r_hT = [R("hT%d" % i) for i in range(5)]
        r_fq = [R("fq%d" % p) for p in range(4)]
        r_kT = [[R("kT%d_%d" % (p, g)) for g in range(4)] for p in range(4)]
        r_V = [R("V%d" % g) for g in range(4)]
        r_G = [R("G%d" % T) for T in range(NT)]
        r_bias = R("biasJ")
        r_ga = [R("ga%d" % c) for c in range(8)]
        r_gr = [R("gr%d" % c) for c in range(8)]
        r_ret = [R("retT%d" % i) for i in range(5)]
        r_act = [R("actT%d" % f) for f in range(22)]
        r_rq = [R("rq%d" % i) for i in range(5)]
        r_rk = [R("rk%d" % i) for i in range(5)]
        r_kdk = [R("kdk%d" % i) for i in range(5)]
        r_rv = [R("rv%d" % i) for i in range(5)]
        r_rgs = [R("rgs%d" % i) for i in range(5)]
        r_fox = [R("fox%d" % i) for i in range(5)]
        r_S32 = [R("S32_%d" % h) for h in range(4)]
        r_Sb = [R("Sb%d" % h) for h in range(4)]
        r_w = [R("w%d" % i) for i in range(NSLOT)]
        r_halo = [R("halo%d" % b) for b in range(44)]
        r_stat = [R("stat%d" % i) for i in range(8)]
        r_m12 = [R("m12_%d" % i) for i in range(4)]
        r_upe = [R("upe%d" % i) for i in range(3)]
        r_uph = [R("uph%d" % i) for i in range(3)]
        r_y0 = [R("y0_%d" % i) for i in range(4)]
        r_y2b = [R("y2b%d" % i) for i in range(3)]
        r_outb = [R("outb%d" % i) for i in range(5)]
        r_out = R("outdram")
        r_xnD = [R("xnD%d" % i) for i in range(2)]

        xnR = Rot([(xn2[i], R("xn%d" % i)) for i in range(2)])
        xnDR = Rot([(xnD[i], r_xnD[i]) for i in range(2)])
        r32R = Rot([(r32[i], R("r32_%d" % i)) for i in range(2)])
        tqR = Rot([(tq[i], R("tq%d" % i)) for i in range(2)])
        ptR = Rot([(ptb[i], R("pt%d" % i)) for i in range(4)])
        o32R = Rot([(o32[i], R("o32_%d" % i)) for i in range(2)])
        aMR = Rot([(aMb[i], R("aM%d" % i)) for i in range(4)])
        r_qkT, r_on32, r_lz, r_gref, r_bn = R("qkT"), R("on32"), R("lz"), R("gref"), R("bn")
        rtokR = Rot([(rtok[i], xnR.items[i][1]) for i in range(2)])
        XR = lambda nm: Res(nm, excl=True)
        pmmR = Rot([(pmm[i], XR("pmm%d" % i)) for i in range(3)])
        r_pT = XR("pT")
        r_pS = [XR("pS%d" % i) for i in range(2)]
        r_pO = [XR("pO%d" % i) for i in range(2)]
        pcum = pSk[0][:, 0:8]
        pgref = pSk[0][:, 8:16]
        r_pcum = r_pS[0]
        r_pgref = r_pS[0]
        pRA = pSk[0][:, 128:256]
        r_pRA = r_pS[0]
        pSt = pSk[1][:, 0:256]
        r_pSt = r_pS[1]
        statR = Rot([(stat[:, 8 * i:8 * i + 8], r_stat[i]) for i in range(8)])

        def C(key, lo=0, hi=None):
            w = CF_W[key]
            hi = w if hi is None else hi
            return cf[:, CF_OFF[key] + lo:CF_OFF[key] + hi]

        identb = cbt[:, 0:128]
        maskT = cbt[:, 128:256]
        indq = cbt[:, 256:768]

        P.dma("sp", cf[:], cf_d, writes=[r_const], key="c0")
        P.dma("sp", cbt[:], cb_d, writes=[r_const], key="c1")
        P.dma("sp", gfb[:], gf_d, writes=[r_const], key="c2")
        P.op("dve", "memset", dict(ap=Vc[:], constant=1.0), writes=r_V)
        P.op("dve", "memset", dict(ap=Vc[:, 0, :, :], constant=0.0), writes=r_V)
        P.op("dve", "memset", dict(ap=Vc[0:16, 0, :, 64:65], constant=1.0), writes=r_V)
        for pi_ in range(4):
            P.op("dve", "memset", dict(ap=ptb[pi_][:], constant=0.0), writes=[ptR.items[pi_][1]])
        P.op("dve", "memset", dict(ap=cpT[:], constant=0.0), writes=[r_bias])
        P.op("dve", "memset", dict(ap=fqT[:], constant=0.0), writes=r_fq)
        for o_i in range(2):
            P.op("dve", "memset", dict(ap=o32[o_i][:], constant=0.0), writes=[o32R.items[o_i][1]])

        def wdesc_list():
            L_ = []
            secs = [0, 512, 1024, 1536, 1544, 2056, 2568, 3080, 3592, 4104, 4616, 5128, 5640, 6152, 6664]
            for i in [2, 0, 1, 3, 4, 5, 6, 7, 8, 9, 10, 11, 12, 13]:
                c0, c1 = secs[i], secs[i + 1]
                L_.append((w_in_d[:, c0:c1].rearrange("(c p) n -> p c n", p=128), 8, c1 - c0))
            L_.append((w_fo_d.rearrange("(c p) n -> p c n", p=128), 4, 1024))
            for hf in range(2):
                L_.append((w_ro_d[:, hf * 512:(hf + 1) * 512].rearrange("(c p) n -> p c n", p=128), 8, 512))
            for hf in range(2):
                L_.append((w_out_d[:, hf * 512:(hf + 1) * 512].rearrange("(c p) n -> p c n", p=128), 8, 512))
            for i in range(11):
                L_.append((w_up_d[:, i * 512:(i + 1) * 512].rearrange("(c p) n -> p c n", p=128), 8, 512))
            for i in range(8):
                L_.append((w_dn_d[:, i * 128:(i + 1) * 128].rearrange("(c p) n -> p c n", p=128), 22, 128))
            return L_

        wdesc = wdesc_list()
        NCH = len(wdesc)
        NGRP = SPC * 4
        total_chunks = NCH * NGRP
        wstate = {"issued": 0, "cur": 0}

        wscr = nc.dram_tensor("wscr", [NCH, 128, WSLOT], BF16, kind="Internal").ap()
        r_scr = [R("scr%d" % i) for i in range(NCH)]

        def w_issue_upto(k):
            while wstate["issued"] <= min(k, total_chunks - 1):
                i = wstate["issued"]
                ci = i % NCH
                src, kc, ncol = wdesc[ci]
                s = i % NSLOT
                if i < NCH:
                    dst = wsl[s][:, 0:kc * ncol].rearrange("p (c n) -> p c n", c=kc)
                    P.dma("pool", dst, src, writes=[r_w[s]], key=("w", s))
                    if NGRP > 1:
                        P.dma("sp", wscr[ci, :, 0:kc * ncol], wsl[s][:, 0:kc * ncol], reads=[r_w[s]], writes=[r_scr[ci]], key=("ws", s))
                else:
                    P.dma("pool", wsl[s][:, 0:kc * ncol], wscr[ci, :, 0:kc * ncol], reads=[r_scr[ci]], writes=[r_w[s]], key=("w", s))
                wstate["issued"] += 1

        def w_next():
            i = wstate["cur"]
            w_issue_upto(i + NSLOT - 1)
            src, kc, ncol = wdesc[i % NCH]
            s = i % NSLOT
            wstate["cur"] += 1
            return wsl[s][:, 0:kc * ncol].rearrange("p (c n) -> p c n", c=kc), r_w[s]

        def mm(out, lhsT, rhs, start, stop, reads, writes):
            P.op("pe", "matmul", dict(out=out, lhsT=lhsT, rhs=rhs, start=start, stop=stop), reads=reads, writes=writes)

        def transp(out, in_, n, reads, writes):
            P.op("pe", "transpose", dict(out=out, in_=in_, identity=identb[0:n, 0:n]),
                 reads=list(reads) + [r_const], writes=writes)

        def rmsnorm_T(ti, n, c0, gkey, xrot):
            st_, rst = statR.next()
            P.op("act", "activation", dict(out=junk[0:n, :], in_=xres[0:n, ti, :], func=AF.Square,
                                               accum_out=st_[0:n, 0:1]), reads=[r_x[ti]], writes=[rst, r_on32])
            P.op("act", "activation", dict(out=st_[0:n, 1:2], in_=st_[0:n, 0:1], func=AF.Sqrt, scale=1.0 / D, bias=EPS),
                 reads=[rst], writes=[rst])
            P.op("dve", "reciprocal", dict(out=st_[0:n, 2:3], in_=st_[0:n, 1:2]), reads=[rst], writes=[rst])
            xn, rxn = xrot.next()
            P.op("dve", "tensor_scalar", dict(out=xn[0:n, :], in0=xres[0:n, ti, :], scalar1=st_[0:n, 2:3], scalar2=None,
                                                  op0=ALU.mult), reads=[r_x[ti], rst], writes=[rxn])
            for c in range(8):
                transp(pT[:, c, 0:n], xn[0:n, c * 128:(c + 1) * 128], n, [rxn], [r_pT])
            g = C(gkey)
            P.op("dve", "tensor_tensor", dict(out=actA[:, :, c0:c0 + n], in0=pT[:, :, 0:n],
                                                  in1=g.unsqueeze(2).to_broadcast([128, 8, n]), op=ALU.mult),
                 reads=[r_pT, r_const], writes=[r_hT[ti]])

        dbg_outs = []

        def dump(name, ap, shape, dt, reads):
            if dbg is None or name not in dbg:
                return
            d = nc.dram_tensor("dbg_" + name, shape, dt, kind="ExternalOutput").ap()
            P.dma("sp", d, ap, reads=reads, writes=[R("dbg" + name)], key="dbg_" + name)
            dbg_outs.append(name)

        gcount = 0
        all_groups = [(sq_, g_) for sq_ in range(SPC) for g_ in range(len(GROUPS))]
        prefetched = set()

        def xload(seq_, T_, slot):
            n_ = tile_n(T_)
            src = meta_d if T_ == 0 else x_d[seq_, 128 * (T_ - 1):128 * T_, :]
            P.dma("sp", xres[0:n_, slot, :], src, writes=[r_x[slot]], key=("x", slot))

        for seq in range(SPC):
            P.op("dve", "memset", dict(ap=halo[:], constant=0.0), writes=r_halo)
            for gi, grp in enumerate(GROUPS):
                if dbg is not None and gcount >= dbg.get("ngroups", 99):
                    break
                gcount += 1
                nt = len(grp)
                ns = [tile_n(T) for T in grp]
                col0 = [sum(ns[:i]) for i in range(nt)]
                G = sum(ns)
                if gi == 0:
                    segs = [(0, NMETA, [0]), (NMETA, G - NMETA, [1, 2, 3, 4])]
                else:
                    segs = [(0, G, [0, 1, 2, 3])]
                kv0 = tile_pos(grp[0])

                P.op("pool", "memset", dict(ap=stat[:, 60:61], constant=0.0),
                     writes=r_m12 + r_upe + r_uph + r_y0 + r_y2b + r_outb + r_xnD + r_rq + r_rk + r_kdk + r_rv + r_rgs + r_act + r_ga + r_gr + r_ret)

                T0 = grp[0]
                P.dma("sp", cosg[:, 0:nt * 64], cos_d[:, T0 * 64:(T0 + nt) * 64], writes=[r_cs], key="cs0")
                P.dma("sp", sing[:, 0:nt * 64], sin_d[:, T0 * 64:(T0 + nt) * 64], writes=[r_cs], key="cs1")
                for ti, T in enumerate(grp):
                    if ti not in prefetched:
                        xload(seq, T, ti)
                prefetched.clear()
                for ti, T in enumerate(grp):
                    rmsnorm_T(ti, ns[ti], col0[ti], "gmix", xnR)

                def fm_proj(W, rW, kc, wc0, src, src_res_of_seg, evac, rot=None):
                    for (s0, sl, tis) in segs:
                        pb, rpb = (rot or pmmR).next()
                        rr = [rW] + src_res_of_seg(tis)
                        for c in range(kc):
                            mm(pb[:, 0:sl], W[:, c, wc0:wc0 + 128], src[:, c, s0:s0 + sl], c == 0, c == kc - 1, rr, [rpb])
                        evac(pb, rpb, s0, sl, tis)

                def tm_proj(W, rW, kc, ncol, src, src_res, ti, evac, wc0=0):
                    n = ns[ti]
                    pb, rpb = pmmR.next()
                    for c in range(kc):
                        mm(pb[0:n, 0:ncol], src[:, c, col0[ti]:col0[ti] + n], W[:, c, wc0:wc0 + ncol], c == 0, c == kc - 1,
                           [rW, src_res], [rpb])
                    evac(pb, rpb, n)

                hres = lambda tis: [r_hT[t] for t in tis]

                W, rW = w_next()
                for ti, T in enumerate(grp):
                    def ev(pb, rpb, n, T=T):
                        P.op("act", "activation", dict(out=Vc[0:n, T, :, 0:64],
                                                           in_=pb[0:n, 0:512].rearrange("p (h d) -> p h d", h=8), func=AF.Copy),
                             reads=[rpb], writes=[r_V[gi]])
                    tm_proj(W, rW, 8, 512, actA, r_hT[ti], ti, ev)
                W, rW = w_next()
                for p in range(4):
                    def ev(pb, rpb, s0, sl, tis, p=p):
                        P.op("act", "activation", dict(out=fqT[0:64, 2 * p, s0:s0 + sl], in_=pb[0:64, 0:sl], func=AF.Copy),
                             reads=[rpb], writes=[r_fq[p]])
                        P.op("act", "activation", dict(out=fqT[64:128, 2 * p + 1, s0:s0 + sl], in_=pb[64:128, 0:sl], func=AF.Copy),
                             reads=[rpb], writes=[r_fq[p]])
                    fm_proj(W, rW, 8, p * 128, actA, hres, ev)
                W, rW = w_next()
                for p in range(4):
                    def ev(pb, rpb, s0, sl, tis, p=p):
                        P.op("act", "activation", dict(out=kTc[:, p, kv0 + s0:kv0 + s0 + sl], in_=pb[:, 0:sl], func=AF.Copy),
                             reads=[rpb], writes=[r_kT[p][gi]])
                    fm_proj(W, rW, 8, p * 128, actA, hres, ev)
                W, rW = w_next()
                r_lzt = [R("lzt%d" % i) for i in range(5)]
                for ti, T in enumerate(grp):
                    def ev(pb, rpb, n, T=T, ti=ti):
                        P.op("dve", "tensor_tensor", dict(out=lzt[0:n, ti, 0:8], in0=pb[0:n, 0:8], in1=C("bfb")[0:n, :], op=ALU.add),
                             reads=[rpb, r_const], writes=[r_lzt[ti]])
                        P.op("act", "activation", dict(out=lzt[0:n, ti, 0:8], in_=lzt[0:n, ti, 0:8], func=AF.Exp, scale=-1.0),
                             reads=[r_lzt[ti]], writes=[r_lzt[ti]])
                        P.op("act", "activation", dict(out=lzt[0:n, ti, 8:16], in_=lzt[0:n, ti, 0:8], func=AF.Ln, bias=1.0),
                             reads=[r_lzt[ti]], writes=[r_lzt[ti]])
                    tm_proj(W, rW, 8, 8, actA, r_hT[ti], ti, ev)

                def cumsum_tile(ti, T):
                    n = ns[ti]
                    first = (T == 0)
                    mm(pcum[0:n, :], C("tri")[0:n, 0:n], lzt[0:n, ti, 8:16], True, first, [r_lzt[ti], r_const], [r_pcum])
                    if not first:
                        npv = tile_n(T - 1)
                        sel = C("sel15") if npv == 16 else C("sel127")
                        mm(pcum[0:n, :], sel[0:npv, 0:n], Gc[0:npv, T - 1, :], False, True, [r_G[T - 1], r_const], [r_pcum])
                    P.op("dve", "tensor_copy", dict(out=Gc[0:n, T, :], in_=pcum[0:n, :]), reads=[r_pcum], writes=[r_G[T]])

                def rotary(pb, rpb, n, ti):
                    ps8 = pb[0:n, 0:512].rearrange("p (g d) -> p g d", g=8)
                    cb_ = cosg[0:n, ti * 64:(ti + 1) * 64].unsqueeze(1).to_broadcast([n, 8, 64])
                    sb_ = sing[0:n, ti * 64:(ti + 1) * 64].unsqueeze(1).to_broadcast([n, 8, 64])
                    r_, rr_ = r32R.next()
                    (ta, rta), (tb, rtb) = tqR.next(), tqR.next()
                    P.op("dve", "tensor_tensor", dict(out=ta[0:n, :].rearrange("p (g d) -> p g d", g=8), in0=ps8, in1=cb_, op=ALU.mult),
                         reads=[rpb, r_cs], writes=[rta])
                    P.op("dve", "tensor_tensor", dict(out=tb[0:n, :].rearrange("p (g d) -> p g d", g=8), in0=ps8, in1=sb_, op=ALU.mult),
                         reads=[rpb, r_cs], writes=[rtb])
                    A4 = ta[0:n, :].rearrange("p (h t d) -> p h t d", h=4, t=2)
                    B4 = tb[0:n, :].rearrange("p (h t d) -> p h t d", h=4, t=2)
                    r4 = r_[0:n, :].rearrange("p (h t d) -> p h t d", h=4, t=2)
                    P.op("dve", "tensor_tensor", dict(out=r4[:, :, 0, :], in0=A4[:, :, 0, :], in1=B4[:, :, 1, :], op=ALU.subtract),
                         reads=[rta, rtb], writes=[rr_])
                    P.op("dve", "tensor_tensor", dict(out=r4[:, :, 1, :], in0=B4[:, :, 0, :], in1=A4[:, :, 1, :], op=ALU.add),
                         reads=[rta, rtb], writes=[rr_])
                    return r_, rr_

                W, rW = w_next()
                for ti, T in enumerate(grp):
                    def ev(pb, rpb, n, ti=ti):
                        r_, rr_ = rotary(pb, rpb, n, ti)
                        for h in range(4):
                            P.op("act", "activation", dict(out=rq[0:n, ti, h * 128:(h + 1) * 128], in_=r_[0:n, h * 128:(h + 1) * 128],
                                                           func=AF.Copy, scale=C("qds")[0:n, h:h + 1]),
                                 reads=[rr_, r_const], writes=[r_rq[ti]])
                    tm_proj(W, rW, 8, 512, actA, r_hT[ti], ti, ev)
                W, rW = w_next()
                for ti, T in enumerate(grp):
                    def ev(pb, rpb, n, ti=ti, T=T):
                        r_, rr_ = rotary(pb, rpb, n, ti)
                        P.op("act", "activation", dict(out=rk[0:n, ti, :], in_=r_[0:n, :], func=AF.Copy),
                             reads=[rr_], writes=[r_rk[ti]])
                        kd = C("kd16") if T == 0 else C("kd128")
                        for h in range(4):
                            P.op("act", "activation", dict(out=kdk[0:n, ti, h * 128:(h + 1) * 128], in_=r_[0:n, h * 128:(h + 1) * 128],
                                                           func=AF.Copy, scale=kd[0:n, h:h + 1]),
                                 reads=[rr_, r_const], writes=[r_kdk[ti]])
                    tm_proj(W, rW, 8, 512, actA, r_hT[ti], ti, ev)
                    cumsum_tile(ti, T)
                for hf in range(2):
                    W, rW = w_next()
                    for ti, T in enumerate(grp):
                        def ev(pb, rpb, n, ti=ti, hf=hf):
                            P.op("act", "activation", dict(out=rv[0:n, ti, hf * 512:(hf + 1) * 512], in_=pb[0:n, 0:512], func=AF.Copy),
                                 reads=[rpb], writes=[r_rv[ti]])
                        tm_proj(W, rW, 8, 512, actA, r_hT[ti], ti, ev)
                for hf in range(2):
                    W, rW = w_next()
                    for ti, T in enumerate(grp):
                        def ev(pb, rpb, n, ti=ti, hf=hf):
                            P.op("act", "activation", dict(out=rgs[0:n, ti, hf * 512:(hf + 1) * 512], in_=pb[0:n, 0:512], func=AF.Silu),
                                 reads=[rpb], writes=[r_rgs[ti]])
                        tm_proj(W, rW, 8, 512, actA, r_hT[ti], ti, ev)
                def ret_stage1(ti_):
                    n_ = ns[ti_]
                    for h in range(4):
                        transp(pT[:, h, 0:n_], rq[0:n_, ti_, h * 128:(h + 1) * 128], n_, [r_rq[ti_]], [r_pT])
                    for h in range(4):
                        transp(pT[:, 4 + h, 0:n_], rk[0:n_, ti_, h * 128:(h + 1) * 128], n_, [r_rk[ti_]], [r_pT])
                    P.op("act", "activation", dict(out=qkT[:, :, 0:n_], in_=pT[:, :, 0:n_], func=AF.Copy), reads=[r_pT], writes=[r_qkT])

                ret_state = {"pend": None}

                def make_ret(ti, T):
                    n = ns[ti]
                    c0 = col0[ti]
                    Mt = C("M16") if T == 0 else C("M128")
                    Mt3 = Mt.rearrange("p (h i) -> p h i", h=4)
                    bankC, rC = pmmR.items[2]
                    por = [(pOk[0], r_pO[0]), (pOk[1], r_pO[1])]
                    Xb = [(pSk[0], r_pS[0]), (pSk[1], r_pS[1])]
                    vvs = [rv[0:n, ti, h * 256:(h + 1) * 256] for h in range(4)]
                    aMs = {}

                    def St(h):
                        mm(bankC[:, (h % 2) * 256:(h % 2) * 256 + 256], kdk[0:n, ti, h * 128:(h + 1) * 128], vvs[h], True, True,
                           [r_kdk[ti], r_rv[ti]], [rC])

                    def Supd(h):
                        src = bankC[:, (h % 2) * 256:(h % 2) * 256 + 256]
                        if T == 0:
                            P.op("dve", "tensor_copy", dict(out=S32[:, h, :], in_=src), reads=[rC], writes=[r_S32[h]])
                        else:
                            P.op("dve", "scalar_tensor_tensor", dict(out=S32[:, h, :], in0=S32[:, h, :], scalar=cd[h], in1=src,
                                                                     op0=ALU.mult, op1=ALU.add), reads=[rC, r_S32[h]], writes=[r_S32[h]])
                        P.op("act", "activation", dict(out=Sb[:, h, :], in_=S32[:, h, :], func=AF.Copy), reads=[r_S32[h]], writes=[r_Sb[h]])

                    def aT(h):
                        pa, rpa = Xb[h % 2]
                        mm(pa[0:n, 0:n], qkT[:, 4 + h, 0:n], qkT[:, h, 0:n], True, True, [r_qkT], [rpa])

                    def aMk(h):
                        pa, rpa = Xb[h % 2]
                        aM, raM = aMR.next()
                        aMs[h] = (aM, raM)
                        P.op("dve", "tensor_tensor", dict(out=aM[0:n, 0:n], in0=pa[0:n, 0:n], in1=Mt3[0:n, h, 0:n], op=ALU.mult),
                             reads=[rpa, r_const], writes=[raM])

                    def omm(h):
                        aM, raM = aMs[h]
                        pb, rpb = por[h // 2]
                        oc = (h % 2) * 256
                        mm(pb[0:n, oc:oc + 256], aM[0:n, 0:n], vvs[h], True, T == 0, [raM, r_rv[ti]], [rpb])
                        if T != 0:
                            mm(pb[0:n, oc:oc + 256], qkT[:, h, 0:n], Sb[:, h, :], False, True, [r_qkT, r_Sb[h]], [rpb])

                    def segA():
                        St(0); St(1)
                        aT(0); aT(1)
                        aMk(0); aMk(1)
                        aT(2); aT(3)
                        aMk(2); aMk(3)

                    def segB():
                        omm(0); omm(1); omm(2); omm(3)
                        Supd(0); Supd(1)
                        St(2); St(3)
                        Supd(2); Supd(3)
                        if ti + 1 < nt:
                            ret_stage1(ti + 1)

                    def segC():
                        if ret_state["pend"] is not None:
                            ret_state["pend"]()
                            ret_state["pend"] = None
                        for h in range(4):
                            pb, rpb = por[h // 2]
                            oc = (h % 2) * 256
                            P.op("dve", "bn_stats", dict(out=bnst[0:n, h, :], in_=pb[0:n, oc:oc + 256]), reads=[rpb], writes=[r_bn])
                            P.op("dve", "bn_aggr", dict(out=bnmv[0:n, h, :], in_=bnst[0:n, h, :]), reads=[r_bn], writes=[r_bn])
                        P.op("act", "activation", dict(out=hrs[0:n, 0:4], in_=bnmv[0:n, :, 1], func=AF.Sqrt, bias=EPS), reads=[r_bn], writes=[r_bn])
                        P.op("dve", "reciprocal", dict(out=hrs[0:n, 4:8], in_=hrs[0:n, 0:4]), reads=[r_bn], writes=[r_bn])
                        P.op("dve", "scalar_tensor_tensor", dict(out=hrs[0:n, 8:12], in0=bnmv[0:n, :, 0], scalar=-1.0, in1=hrs[0:n, 4:8],
                                                                 op0=ALU.mult, op1=ALU.mult), reads=[r_bn], writes=[r_bn])
                        for h in range(4):
                            pb, rpb = por[h // 2]
                            oc = (h % 2) * 256
                            P.op("act", "activation", dict(out=on16[0:n, h * 256:(h + 1) * 256], in_=pb[0:n, oc:oc + 256], func=AF.Identity,
                                                           scale=hrs[0:n, 4 + h:5 + h], bias=hrs[0:n, 8 + h:9 + h]),
                                 reads=[rpb, r_bn], writes=[r_on32])
                        rt_, rrt = rtokR.next()
                        P.op("pool", "tensor_tensor", dict(out=rt_[0:n, :], in0=on16[0:n, :], in1=rgs[0:n, ti, :], op=ALU.mult),
                             reads=[r_on32, r_rgs[ti]], writes=[rrt])

                        def fin_ret():
                            for c in range(8):
                                transp(pT[:, c, 0:n], rt_[0:n, c * 128:(c + 1) * 128], n, [rrt], [r_pT])
                            P.op("act", "activation", dict(out=retT[:, :, c0:c0 + n], in_=pT[:, :, 0:n], func=AF.Copy), reads=[r_pT], writes=[r_ret[ti]])
                        ret_state["pend"] = fin_ret
                    return [segA, segB, segC]

                ret_stage1(0)
                ret_segs = []
                for ti, T in enumerate(grp):
                    ret_segs += make_ret(ti, T)
                gate_rot = Rot([pmmR.items[0], pmmR.items[1]])

                for which, (gT_, rg_) in enumerate([(gaT, r_ga), (grT, r_gr)]):
                    for hf in range(2):
                        W, rW = w_next()
                        for b4 in range(4):
                            blk = hf * 4 + b4
                            def ev(pb, rpb, s0, sl, tis, blk=blk, gT_=gT_, rg_=rg_, which=which):
                                bcol = C("bbr")[:, which * 8 + blk:which * 8 + blk + 1]
                                P.op("act", "activation", dict(out=gT_[:, blk, s0:s0 + sl], in_=pb[:, 0:sl], func=AF.Sigmoid, bias=bcol),
                                     reads=[rpb, r_const], writes=[rg_[blk]])
                            fm_proj(W, rW, 8, b4 * 128, actA, hres, ev, rot=gate_rot)
                            if ret_segs:
                                ret_segs.pop(0)()
                while ret_segs:
                    ret_segs.pop(0)()

                if gcount == 1:
                    dump("hT", actA[:, :, 0:G], [128, 8, G], BF16, r_hT[:nt])
                    dump("fqT", fqT[:, :, 0:G], [128, 8, G], BF16, r_fq)
                    dump("kT", kTc[:, :, 0:G], [128, 4, G], BF16, [r_kT[p][0] for p in range(4)])
                    dump("V", Vc[:, 0:5, :, :], [128, 5, 8, 65], BF16, [r_V[0]])
                    dump("G", Gc[:, 0:5, :], [128, 5, 8], F32, r_G[:5])
                    dump("rq", rq[:, :, :], [128, 5, 512], BF16, r_rq)
                    dump("rk", rk[:, :, :], [128, 5, 512], BF16, r_rk)
                    dump("kdk", kdk[:, :, :], [128, 5, 512], BF16, r_kdk)
                    dump("rv", rv[:, :, :], [128, 5, 1024], BF16, r_rv)
                    dump("rgs", rgs[:, :, :], [128, 5, 1024], BF16, r_rgs)
                    dump("gaT", gaT[:, :, 0:G], [128, 8, G], BF16, r_ga)
                    dump("grT", grT[:, :, 0:G], [128, 8, G], BF16, r_gr)
                qsets = [[0], [1, 2, 3, 4]] if gi == 0 else [[0, 1, 2, 3]]
                for Q in qsets:
                    NQ = sum(ns[t_] for t_ in Q)
                    qc0 = col0[Q[0]]
                    Tq = [grp[t_] for t_ in Q]
                    qcs = [sum(ns[t_] for t_ in Q[:i_]) for i_ in range(len(Q))]
                    for qi, ti in enumerate(Q):
                        T = grp[ti]
                        n = ns[ti]
                        selv = C("selm")[0:n, 0:4] if T == 0 else C("selq")[0:n, qi * 4:qi * 4 + 4]
                        mm(pgref[0:4, :], selv, Gc[0:n, T, :], qi == 0, qi == len(Q) - 1, [r_G[T], r_const], [r_pgref])
                    gv, gr1, gr2 = gpc[0:4, 0:8], gpc[0:4, 8:16], gpc[0:4, 16:24]
                    ghi, gmid, glo = gpb[0:4, 0:8], gpb[0:4, 8:16], gpb[0:4, 16:24]
                    P.op("dve", "tensor_scalar", dict(out=gv, in0=pgref[0:4, :], scalar1=-8.0, scalar2=None, op0=ALU.mult), reads=[r_pgref], writes=[r_gref])
                    P.op("dve", "tensor_copy", dict(out=ghi, in_=gv), reads=[r_gref], writes=[r_gref])
                    P.op("dve", "tensor_tensor", dict(out=gr1, in0=gv, in1=ghi, op=ALU.subtract), reads=[r_gref], writes=[r_gref])
                    P.op("dve", "tensor_copy", dict(out=gmid, in_=gr1), reads=[r_gref], writes=[r_gref])
                    P.op("dve", "tensor_tensor", dict(out=gr2, in0=gr1, in1=gmid, op=ALU.subtract), reads=[r_gref], writes=[r_gref])
                    P.op("dve", "tensor_copy", dict(out=glo, in_=gr2), reads=[r_gref], writes=[r_gref])
                    for pi, gp_ in enumerate([ghi, gmid, glo]):
                        P.op("dve", "tensor_copy", dict(out=cpT[32 * pi:32 * pi + 4, :, :], in_=gp_.unsqueeze(2).to_broadcast([4, 8, 128])),
                             reads=[r_gref], writes=[r_bias])
                    Tmax = Tq[-1]
                    blocks = [(h, kt) for h in range(8) for kt in range(Tmax + 1)]
                    sbanks = [(pSk[0], r_pS[0]), (pSk[1], r_pS[1]), pmmR.items[0], pmmR.items[1]]
                    NB = len(blocks)
                    pts, pend = {}, {}

                    def split(kt):
                        i0 = min(i_ for i_ in range(len(Q)) if Tq[i_] >= kt)
                        cs = qcs[i0]
                        diag = (Tq[i0] == kt)
                        nd = ns[Q[i0]]
                        return i0, cs, diag, nd

                    def S_stage(bi):
                        h, kt = blocks[bi]
                        p, base = h // 2, 64 * (h % 2)
                        nk = tile_n(kt)
                        kc0 = tile_pos(kt)
                        kg = 0 if kt <= 4 else (kt - 1) // 4
                        i0, cs, diag, nd = split(kt)
                        bank, rbank = sbanks[bi % 4]
                        kTv = kTc[:, p, kc0:kc0 + nk]
                        rr = [r_kT[p][kg], r_fq[p]]
                        cpv = cpT[:, h, 0:nk]
                        if diag:
                            mm(bank[0:nk, cs:cs + nd], kTv, fqT[:, h, qc0 + cs:qc0 + cs + nd], True, False, rr, [rbank])
                            mm(bank[0:nk, cs:cs + nd], identb[0:nk, 0:nk], maskT[0:nk, 0:nd], False, False, [r_const], [rbank])
                            mm(bank[0:nk, cs:cs + nd], cpv, indq[:, cs:cs + nd], False, True, [r_const, r_bias], [rbank])
                            if cs + nd < NQ:
                                mm(bank[0:nk, cs + nd:NQ], kTv, fqT[:, h, qc0 + cs + nd:qc0 + NQ], True, False, rr, [rbank])
                                mm(bank[0:nk, cs + nd:NQ], cpv, indq[:, cs + nd:NQ], False, True, [r_const, r_bias], [rbank])
                        else:
                            mm(bank[0:nk, cs:NQ], kTv, fqT[:, h, qc0 + cs:qc0 + NQ], True, False, rr, [rbank])
                            mm(bank[0:nk, cs:NQ], cpv, indq[:, cs:NQ], False, True, [r_const, r_bias], [rbank])
                        pt, rpt = ptR.next()
                        P.op("act", "activation", dict(
                            out=pt[0:nk, cs:NQ], in_=bank[0:nk, cs:NQ], func=AF.Exp, bias=Gc[0:nk, kt, h:h + 1], scale=0.125),
                             reads=[rbank, r_G[kt]], writes=[rpt])
                        pts[bi] = (pt, rpt)

                    def PV_stage(bi, step):
                        h, kt = blocks[bi]
                        p, base = h // 2, 64 * (h % 2)
                        nk = tile_n(kt)
                        kg = 0 if kt <= 4 else (kt - 1) // 4
                        i0, cs, diag, nd = split(kt)
                        po, rpo = pOk[h % 2], r_pO[h % 2]
                        pt, rpt = pts.pop(bi)
                        nkp = 128 if kt == 0 else nk
                        Vv = Vc[0:nkp, kt, h, 0:65]
                        if diag:
                            mm(po[0:65, cs:cs + nd], Vv, pt[0:nkp, cs:cs + nd], kt == 0, True, [r_V[kg], rpt], [rpo])
                            if cs + nd < NQ:
                                mm(po[0:65, cs + nd:NQ], Vv, pt[0:nkp, cs + nd:NQ], kt == 0, False, [r_V[kg], rpt], [rpo])
                        else:
                            mm(po[0:65, cs:NQ], Vv, pt[0:nkp, cs:NQ], kt == 0, False, [r_V[kg], rpt], [rpo])
                        if kt == Tmax:
                            o_, ro_ = o32R.next()
                            P.op("act", "activation", dict(out=o_[0:65, 0:NQ], in_=po[0:65, 0:NQ], func=AF.Copy),
                                 reads=[rpo], writes=[ro_])
                            P.op("dve", "reciprocal", dict(out=o_[64:65, 0:NQ], in_=o_[64:65, 0:NQ]), reads=[ro_], writes=[ro_])

                            def norm2(o_=o_, ro_=ro_, p=p, base=base):
                                pr, rpr = pmmR.items[2]
                                mm(pr[0:64, 0:NQ], C("ones")[:, 0:64], o_[:, 0:NQ], True, True, [ro_, r_const], [rpr])
                                P.op("dve", "tensor_tensor", dict(
                                    out=foxT[base:base + 64, p, qc0:qc0 + NQ], in0=o_[0:64, 0:NQ], in1=pr[0:64, 0:NQ], op=ALU.mult),
                                     reads=[ro_, rpr], writes=[r_fox[t_] for t_ in Q])
                            pend.setdefault(step + min(12, 2 * (Tmax + 1)), []).append(norm2)

                    LA = 3
                    for step in range(NB + LA + 14):
                        for f_ in pend.pop(step, []):
                            f_()
                        if step < NB:
                            S_stage(step)
                        if 0 <= step - LA < NB:
                            PV_stage(step - LA, step)
                    assert not pend and not pts

                if gcount == 1:
                    dump("foxT", foxT[:, :, 0:G], [128, 4, G], BF16, r_fox)
                    dump("retT", retT[:, :, 0:G], [128, 8, G], BF16, r_ret)
                P.op("pool", "memset", dict(ap=stat[:, 61:62], constant=0.0),
                     writes=r_m12 + r_upe + r_uph + r_y0 + r_y2b + r_outb + r_xnD + r_rq + r_rk + r_kdk + r_rv + r_rgs)

                Wfo, rWfo = w_next()
                m12R = Rot([(m12[i], r_m12[i]) for i in range(4)])
                fres = lambda tis: [r_fox[t] for t in tis]
                rres = lambda tis: [r_ret[t] for t in tis]
                for cb_ in range(8):
                    for (s0, sl, tis) in segs:
                        pa1, rp1 = pmmR.next()
                        for c in range(4):
                            mm(pa1[:, 0:sl], Wfo[:, c, cb_ * 128:(cb_ + 1) * 128], foxT[:, c, s0:s0 + sl], c == 0, c == 3, [rWfo] + fres(tis), [rp1])
                        P.op("dve", "tensor_tensor", dict(out=actA[:, cb_, s0:s0 + sl], in0=pa1[:, 0:sl], in1=gaT[:, cb_, s0:s0 + sl], op=ALU.mult),
                             reads=[rp1, r_ga[cb_]], writes=[r_hT[t] for t in tis])
                if ret_state["pend"] is not None:
                    ret_state["pend"]()
                    ret_state["pend"] = None
                for hf in range(2):
                    Wro, rWro = w_next()
                    for cb_ in range(hf * 4, hf * 4 + 4):
                        for (s0, sl, tis) in segs:
                            pa2, rp2 = pmmR.next()
                            for c in range(8):
                                mm(pa2[:, 0:sl], Wro[:, c, (cb_ % 4) * 128:(cb_ % 4 + 1) * 128], retT[:, c, s0:s0 + sl], c == 0, c == 7, [rWro] + rres(tis), [rp2])
                            m2, rm2 = m12R.next()
                            P.op("dve", "tensor_tensor", dict(out=m2[:, 0:sl], in0=pa2[:, 0:sl], in1=grT[:, cb_, s0:s0 + sl], op=ALU.mult),
                                 reads=[rp2, r_gr[cb_]], writes=[rm2])
                            P.op("dve", "tensor_tensor", dict(out=actA[:, cb_, s0:s0 + sl], in0=m2[:, 0:sl], in1=actA[:, cb_, s0:s0 + sl], op=ALU.add),
                                 reads=[rm2] + [r_hT[t] for t in tis], writes=[r_hT[t] for t in tis])
                for hf in range(2):
                    W, rW = w_next()
                    for ti, T in enumerate(grp):
                        def ev(pb, rpb, n, ti=ti, hf=hf):
                            P.op("dve", "tensor_tensor", dict(out=xres[0:n, ti, hf * 512:(hf + 1) * 512], in0=pb[0:n, 0:512],
                                                                  in1=xres[0:n, ti, hf * 512:(hf + 1) * 512], op=ALU.add),
                                 reads=[rpb, r_x[ti]], writes=[r_x[ti]])
                        tm_proj(W, rW, 8, 512, actA, r_hT[ti], ti, ev)

                if gcount == 1:
                    dump("x1", xres[:, :, :], [128, 5, D], F32, r_x)
                P.op("pool", "memset", dict(ap=stat[:, 62:63], constant=0.0), writes=r_act + r_ga + r_gr + r_ret)

                for ti, T in enumerate(grp):
                    rmsnorm_T(ti, ns[ti], col0[ti], "gffn", xnDR)
                cw3 = C("cw").rearrange("p (b k) -> p b k", k=3)
                cbv = C("cbias")
                upR = Rot([(upe[i], r_upe[i], r_uph[i]) for i in range(3)])
                y0R = Rot([(y0b[i], r_y0[i]) for i in range(4)])
                y2R = Rot([(y2b[i], r_y2b[i]) for i in range(3)])
                W, rW = None, None
                fin = {}
                SKEW = 2
                for b in range(44 + SKEW):
                    if b < 44:
                        if b % 4 == 0:
                            W, rW = w_next()
                        f = b % 22
                        up_, rup, ruh = upR.next()
                        y0_, ry0 = y0R.next()
                        P.op("dve", "tensor_copy", dict(out=up_[:, 0:2], in_=halo[:, b, :]), reads=[r_halo[b]], writes=[ruh])
                        for (s0, sl, tis) in segs:
                            pb, rpb = pmmR.next()
                            for c in range(8):
                                mm(pb[:, 0:sl], W[:, c, (b % 4) * 128:(b % 4 + 1) * 128], actA[:, c, s0:s0 + sl], c == 0, c == 7, [rW] + hres(tis), [rpb])
                            P.op("act", "activation", dict(out=up_[:, 2 + s0:2 + s0 + sl], in_=pb[:, 0:sl], func=AF.Copy),
                                 reads=[rpb], writes=[rup])
                            P.op("act", "activation", dict(out=y0_[:, s0:s0 + sl], in_=pb[:, 0:sl], func=AF.Identity,
                                                           scale=cw3[:, b, 2:3], bias=cbv[:, b:b + 1]),
                                 reads=[rpb, r_const], writes=[ry0])
                        P.op("dve", "scalar_tensor_tensor", dict(out=y0_[:, 0:G], in0=up_[:, 1:1 + G], scalar=cw3[:, b, 1:2], in1=y0_[:, 0:G],
                                                                 op0=ALU.mult, op1=ALU.add), reads=[rup, ruh, ry0, r_const], writes=[ry0])
                        if b < 22:
                            yo_, ryo = y0_, ry0
                        else:
                            yo_, ryo = y2R.next()
                        P.op("dve", "scalar_tensor_tensor", dict(out=yo_[:, 0:G], in0=up_[:, 0:G], scalar=cw3[:, b, 0:1], in1=y0_[:, 0:G],
                                                                 op0=ALU.mult, op1=ALU.add), reads=[rup, ruh, ry0, r_const], writes=[ryo] if b >= 22 else [ry0])
                        P.op("dve", "tensor_copy", dict(out=halo[:, b, :], in_=up_[:, G:G + 2]), reads=[rup], writes=[r_halo[b]])
                        fin[b] = (yo_, ryo, f)
                    bb = b - SKEW
                    if bb >= 0:
                        y0_, ry0, f = fin.pop(bb)
                        if bb < 22:
                            P.op("act", "activation", dict(out=actT[:, f, 0:G], in_=y0_[:, 0:G], func=AF.Silu), reads=[ry0], writes=[r_act[f]])
                        else:
                            P.op("dve", "tensor_tensor", dict(out=actT[:, f, 0:G], in0=y0_[:, 0:G], in1=actT[:, f, 0:G], op=ALU.mult),
                                 reads=[ry0, r_act[f]], writes=[r_act[f]])
                P.op("pool", "memset", dict(ap=stat[:, 63:64], constant=0.0), writes=r_m12 + r_upe + r_uph + r_y0 + r_outb)
                gidx = all_groups.index((seq, gi))
                nxt = all_groups[gidx + 1] if gidx + 1 < len(all_groups) else None
                if dbg is not None and gcount >= dbg.get("ngroups", 99):
                    nxt = None

                def final_tile(ti, T):
                    n = ns[ti]
                    if T != 0:
                        st_, rst = statR.next()
                        P.op("act", "activation", dict(out=junk[0:n, :], in_=xres[0:n, ti, :], func=AF.Square, accum_out=st_[0:n, 0:1]),
                             reads=[r_x[ti]], writes=[rst, r_on32])
                        P.op("act", "activation", dict(out=st_[0:n, 1:2], in_=st_[0:n, 0:1], func=AF.Sqrt, scale=1.0 / D, bias=EPS),
                             reads=[rst], writes=[rst])
                        P.op("dve", "reciprocal", dict(out=st_[0:n, 2:3], in_=st_[0:n, 1:2]), reads=[rst], writes=[rst])
                        ob, rob = outb[ti], r_outb[ti]
                        P.op("dve", "scalar_tensor_tensor", dict(out=ob[0:n, :], in0=xres[0:n, ti, :], scalar=st_[0:n, 2:3], in1=gfb[0:n, :],
                                                                 op0=ALU.mult, op1=ALU.mult), reads=[r_x[ti], rst, r_const], writes=[rob])
                        P.dma("sp", out_d[seq, 128 * (T - 1):128 * T, :], ob[0:n, :], reads=[rob], writes=[R("outdram")], key=("o", ti))
                    if nxt is not None:
                        ngrp = GROUPS[nxt[1]]
                        if ti < len(ngrp):
                            xload(nxt[0], ngrp[ti], ti)
                            prefetched.add(ti)

                for q8 in range(8):
                    W, rW = w_next()
                    for ti, T in enumerate(grp):
                        n = ns[ti]
                        pb, rpb = pmmR.next()
                        for f in range(22):
                            mm(pb[0:n, 0:128], actT[:, f, col0[ti]:col0[ti] + n], W[:, f, 0:128], f == 0, f == 21, [rW, r_act[f]], [rpb])
                        P.op("dve", "tensor_tensor", dict(out=xres[0:n, ti, q8 * 128:(q8 + 1) * 128], in0=pb[0:n, 0:128],
                                                          in1=xres[0:n, ti, q8 * 128:(q8 + 1) * 128], op=ALU.add),
                             reads=[rpb, r_x[ti]], writes=[r_x[ti]])
                        if q8 == 7:
                            if gcount == 1 and ti == nt - 1:
                                dump("x2", xres[:, :, :], [128, 5, D], F32, r_x)
                            final_tile(ti, T)
        P.emit()
    nc._n_ops = len(P.ops)
    return nc


def _prep_inputs(x, meta, norm_mix_g, w_in, b_forget, b_branch, w_fox_o, w_ret_o, w_out,
                 norm_ffn_g, w_up, conv_w, conv_b, w_down, norm_f_g):
    f = lambda a: np.ascontiguousarray(np.asarray(a, dtype=np.float32))
    cfh, cbh = make_consts()
    cfa = np.zeros((128, CF_TOT), np.float32)
    for k in ["tri", "sel0", "selq", "selm", "sel15", "sel127", "ones", "qds", "kd128", "kd16", "M128", "M16"]:
        cfa[:, CF_OFF[k]:CF_OFF[k] + CF_W[k]] = cfh[k]
    cfa[:, CF_OFF["gmix"]:CF_OFF["gmix"] + 8] = f(norm_mix_g)[0].reshape(8, 128).T
    cfa[:, CF_OFF["gffn"]:CF_OFF["gffn"] + 8] = f(norm_ffn_g)[0].reshape(8, 128).T
    cfa[:, CF_OFF["bfb"]:CF_OFF["bfb"] + 8] = np.broadcast_to(f(b_forget)[0][None, :], (128, 8))
    cfa[:, CF_OFF["bbr"]:CF_OFF["bbr"] + 16] = f(b_branch)[0].reshape(16, 128).T
    cw = f(conv_w)[0]
    cfa[:, CF_OFF["cw"]:CF_OFF["cw"] + 132] = cw.reshape(3, 44, 128).transpose(2, 1, 0).reshape(128, 132)
    cfa[:, CF_OFF["cbias"]:CF_OFF["cbias"] + 44] = f(conv_b)[0].reshape(44, 128).T
    cba = np.concatenate([cbh["ident"], cbh["maskT"], cbh["ind"]], axis=1)
    common = dict(meta=f(meta), w_in=f(w_in)[0], w_fox_o=f(w_fox_o)[0], w_ret_o=f(w_ret_o)[0], w_out=f(w_out)[0],
                  w_up=f(w_up)[0], w_down=f(w_down)[0], cf=cfa, cb=np.ascontiguousarray(cba),
                  gfb=np.ascontiguousarray(np.broadcast_to(f(norm_f_g)[None, :], (128, D))),
                  cos=cfh["cos"], sin=cfh["sin"])
    xs = f(x)
    in_maps = []
    for c in range(NCORES):
        m = dict(common)
        m["x"] = np.ascontiguousarray(xs[c * SPC:(c + 1) * SPC])
        in_maps.append(m)
    return in_maps


def kernel(**inputs):
    in_maps = _prep_inputs(**inputs)
    nc = build_nc()
    res = run_bass_kernel_spmd(nc, in_maps, core_ids=list(range(NCORES)))
    out = np.concatenate([np.asarray(r["out"]) for r in res.results], axis=0)
    return out.astype(np.float32)
```
